# Optimizing a Trainium2 kernel written in Bass

```python
import math
import jax, jax.numpy as jnp
from jax import lax
import numpy as np


D_MODEL = 1024
BATCH = 8
SEQ = 8192
DEPTH = 1
DEC_BATCH = 8
DEC_SEQ = 16
PAST_LEN = 2048

CHUNK = 64
GMLP_CHUNK = 128
GMLP_HEADS = 4
GMLP_HEAD_DIM = 128
GMLP_WIDTH = GMLP_HEADS * GMLP_HEAD_DIM
MLA_HEADS = 4
QK_NOPE_DIM = 128
QK_ROPE_DIM = 64
V_HEAD_DIM = 128
Q_LORA_RANK = 384
KV_LORA_RANK = 256
MLA_WIDTH = MLA_HEADS * V_HEAD_DIM
MIX_WIDTH = GMLP_WIDTH + MLA_WIDTH
IN_SPLITS = (GMLP_WIDTH, 2 * GMLP_WIDTH, 2 * GMLP_WIDTH + Q_LORA_RANK,
             2 * GMLP_WIDTH + Q_LORA_RANK + KV_LORA_RANK)
IN_WIDTH = 2 * GMLP_WIDTH + Q_LORA_RANK + KV_LORA_RANK + QK_ROPE_DIM
D_FF = 2816
CONV_W = 3
ROPE_BASE = 10000.0
Q_BLOCK = 128
ATTN_SCALE = 1.0 / math.sqrt(QK_NOPE_DIM + QK_ROPE_DIM)
DEEPNORM_ALPHA = (2.0 * DEPTH) ** 0.25
DEEPNORM_BETA = (8.0 * DEPTH) ** -0.25
LN_EPS = 1e-5
RMS_EPS = 1e-6
NEG_INF = -1e30

kernel_name = 'hymba_gmlp_mla_convffn_deepnorm_stream_step'


def layer_norm(x, g, b):
    xf = x.astype(jnp.float32)
    mu = jnp.mean(xf, axis=-1, keepdims=True)
    var = jnp.mean(jnp.square(xf - mu), axis=-1, keepdims=True)
    y = (xf - mu) * lax.rsqrt(var + LN_EPS)
    return (y * g.astype(jnp.float32) + b.astype(jnp.float32)).astype(x.dtype)


def rms_norm(x, g):
    xf = x.astype(jnp.float32)
    y = xf * lax.rsqrt(jnp.mean(jnp.square(xf), axis=-1, keepdims=True) + RMS_EPS)
    return (y * g.astype(jnp.float32)).astype(x.dtype)


def rope_tables(pos):
    inv_freq = jnp.power(ROPE_BASE, -jnp.arange(0, QK_ROPE_DIM, 2, dtype=jnp.float32) / QK_ROPE_DIM)
    ang = pos.astype(jnp.float32)[:, None] * inv_freq[None, :]
    return jnp.cos(ang), jnp.sin(ang)


def apply_rope(x, cos, sin):
    half = x.shape[-1] // 2
    x1, x2 = x[..., :half], x[..., half:]
    c = cos.astype(x.dtype)
    s = sin.astype(x.dtype)
    return jnp.concatenate([x1 * c - x2 * s, x2 * c + x1 * s], axis=-1)


def gmlp_spatial(v, w_s, b_s):
    bsz, s_len = v.shape[0], v.shape[1]
    n = min(s_len, GMLP_CHUNK)
    blk = jnp.arange(n) // CHUNK
    mask = blk[None, :] <= blk[:, None]
    w = jnp.where(mask[None], w_s[:, :n, :n], 0.0)
    vc = v.reshape(bsz, s_len // n, n, GMLP_HEADS, GMLP_HEAD_DIM)
    s = jnp.einsum('hij,bcjhd->bcihd', w, vc) + b_s[:, :n].T[:, :, None]
    return s.reshape(bsz, s_len, GMLP_HEADS, GMLP_HEAD_DIM)


def mla_attention(q_lat, q_pe, ckv, kpe, q_offset):
    bsz, s_len = q_lat.shape[0], q_lat.shape[1]
    qb = min(s_len, Q_BLOCK)
    nb = s_len // qb
    k_chunk = jnp.arange(ckv.shape[1]) // CHUNK
    q_pos = q_offset + jnp.arange(s_len)

    def attend_block(args):
        ql, qp, qpos = args
        sc = jnp.einsum('bqhc,bkc->bhqk', ql, ckv) + jnp.einsum('bqhr,bkr->bhqk', qp, kpe)
        sc = sc.astype(jnp.float32) * ATTN_SCALE
        visible = k_chunk[None, :] <= (qpos // CHUNK)[:, None]
        sc = jnp.where(visible[None, None], sc, NEG_INF)
        p = jax.nn.softmax(sc, axis=-1).astype(ckv.dtype)
        return jnp.einsum('bhqk,bkc->bqhc', p, ckv)

    xs = (q_lat.reshape(bsz, nb, qb, MLA_HEADS, KV_LORA_RANK).swapaxes(0, 1),
          q_pe.reshape(bsz, nb, qb, MLA_HEADS, QK_ROPE_DIM).swapaxes(0, 1),
          q_pos.reshape(nb, qb))
    o = lax.map(attend_block, xs)
    return o.swapaxes(0, 1).reshape(bsz, s_len, MLA_HEADS, KV_LORA_RANK)


def token_mixer(x, ckv_past, kpe_past, w_in, ln_v_g, ln_v_b, w_s, b_s, g_q, w_uq,
                g_kv, w_uk, w_uv, w_o):
    bsz, s_len = x.shape[0], x.shape[1]
    past = 0 if ckv_past is None else ckv_past.shape[1]
    proj = jnp.einsum('bsd,de->bse', x, w_in)
    u, v, cq, ckv, kpe = jnp.split(proj, list(IN_SPLITS), axis=-1)
    u = jax.nn.gelu(u)
    v = layer_norm(jax.nn.gelu(v), ln_v_g, ln_v_b)
    sgu = gmlp_spatial(v.reshape(bsz, s_len, GMLP_HEADS, GMLP_HEAD_DIM), w_s, b_s)
    gm = u * sgu.reshape(bsz, s_len, GMLP_WIDTH)
    cos, sin = rope_tables(past + jnp.arange(s_len))
    q = jnp.einsum('bsr,rhd->bshd', rms_norm(cq, g_q), w_uq)
    q_nope = q[..., :QK_NOPE_DIM]
    q_pe = apply_rope(q[..., QK_NOPE_DIM:], cos[:, None, :], sin[:, None, :])
    ckv = rms_norm(ckv, g_kv)
    kpe = apply_rope(kpe, cos, sin)
    if ckv_past is None:
        ckv_all, kpe_all = ckv, kpe
    else:
        ckv_all = jnp.concatenate([ckv_past, ckv], axis=1)
        kpe_all = jnp.concatenate([kpe_past, kpe], axis=1)
    q_lat = jnp.einsum('bshd,chd->bshc', q_nope, w_uk)
    o_lat = mla_attention(q_lat, q_pe, ckv_all, kpe_all, past)
    mla = jnp.einsum('bshc,chd->bshd', o_lat, w_uv).reshape(bsz, s_len, MLA_WIDTH)
    out = jnp.einsum('bse,ed->bsd', jnp.concatenate([gm, mla], axis=-1), w_o)
    return out, ckv, kpe, v


def conv_ffn(x, conv_past, w_up, w_conv, b_conv, w_down):
    bsz, s_len = x.shape[0], x.shape[1]
    up = jnp.einsum('bsd,df->bsf', x, w_up)
    if conv_past is None:
        conv_past = jnp.zeros((bsz, CONV_W - 1, up.shape[-1]), up.dtype)
    padded = jnp.concatenate([conv_past, up], axis=1)
    conv = b_conv
    for k in range(CONV_W):
        conv = conv + w_conv[k] * padded[:, k:k + s_len]
    a, g = jnp.split(conv, 2, axis=-1)
    out = jnp.einsum('bsf,fd->bsd', jax.nn.silu(a) * g, w_down)
    return out, padded[:, padded.shape[1] - (CONV_W - 1):]


def trunk_layer(x, ckv_past, kpe_past, conv_past, w_in, ln_v_g, ln_v_b, w_s, b_s, g_q, w_uq,
                g_kv, w_uk, w_uv, w_o, ln1_g, ln1_b, w_up, w_conv, b_conv, w_down, ln2_g, ln2_b):
    mix, ckv, kpe, v = token_mixer(x, ckv_past, kpe_past, w_in, ln_v_g, ln_v_b, w_s, b_s,
                                   g_q, w_uq, g_kv, w_uk, w_uv, w_o)
    h = layer_norm(DEEPNORM_ALPHA * x + mix, ln1_g, ln1_b)
    ff, conv_state = conv_ffn(h, conv_past, w_up, w_conv, b_conv, w_down)
    y = layer_norm(DEEPNORM_ALPHA * h + ff, ln2_g, ln2_b)
    return y, ckv, kpe, v, conv_state


def setup_inputs(seed: int = 0) -> dict:
    key = jax.random.key(seed)
    ks = jax.random.split(key, 26)

    def nrm(k, shape, scale):
        return jax.random.normal(k, shape, jnp.float32) * scale

    L = DEPTH
    return {
        'x_prompt': nrm(ks[0], (BATCH, SEQ, D_MODEL), 1.0),
        'x_sample': nrm(ks[1], (DEC_BATCH, DEC_SEQ, D_MODEL), 1.0),
        'cache_ckv': nrm(ks[2], (L, DEC_BATCH, PAST_LEN, KV_LORA_RANK), 1.0),
        'cache_kpe': nrm(ks[3], (L, DEC_BATCH, PAST_LEN, QK_ROPE_DIM), 1.0),
        'state_ffn_conv': nrm(ks[4], (L, DEC_BATCH, CONV_W - 1, 2 * D_FF), 1.0),
        'w_in': nrm(ks[5], (L, D_MODEL, IN_WIDTH), D_MODEL ** -0.5),
        'ln_v_g': 1.0 + nrm(ks[6], (L, GMLP_WIDTH), 0.02),
        'ln_v_b': nrm(ks[7], (L, GMLP_WIDTH), 0.02),
        'w_s': nrm(ks[8], (L, GMLP_HEADS, GMLP_CHUNK, GMLP_CHUNK), GMLP_CHUNK ** -0.5),
        'b_s': 1.0 + nrm(ks[9], (L, GMLP_HEADS, GMLP_CHUNK), 0.02),
        'g_q': 1.0 + nrm(ks[10], (L, Q_LORA_RANK), 0.02),
        'w_uq': nrm(ks[11], (L, Q_LORA_RANK, MLA_HEADS, QK_NOPE_DIM + QK_ROPE_DIM), Q_LORA_RANK ** -0.5),
        'g_kv': 1.0 + nrm(ks[12], (L, KV_LORA_RANK), 0.02),
        'w_uk': nrm(ks[13], (L, KV_LORA_RANK, MLA_HEADS, QK_NOPE_DIM), KV_LORA_RANK ** -0.5),
        'w_uv': nrm(ks[14], (L, KV_LORA_RANK, MLA_HEADS, V_HEAD_DIM), KV_LORA_RANK ** -0.5),
        'w_o': nrm(ks[15], (L, MIX_WIDTH, D_MODEL), MIX_WIDTH ** -0.5 * DEEPNORM_BETA),
        'ln1_g': 1.0 + nrm(ks[16], (L, D_MODEL), 0.02),
        'ln1_b': nrm(ks[17], (L, D_MODEL), 0.02),
        'w_up': nrm(ks[18], (L, D_MODEL, 2 * D_FF), D_MODEL ** -0.5),
        'w_conv': nrm(ks[19], (L, CONV_W, 2 * D_FF), CONV_W ** -0.5),
        'b_conv': nrm(ks[20], (L, 2 * D_FF), 0.02),
        'w_down': nrm(ks[21], (L, D_FF, D_MODEL), D_FF ** -0.5 * DEEPNORM_BETA),
        'ln2_g': 1.0 + nrm(ks[22], (L, D_MODEL), 0.02),
        'ln2_b': nrm(ks[23], (L, D_MODEL), 0.02),
    }


def reference(x_prompt, x_sample, cache_ckv, cache_kpe, state_ffn_conv, w_in, ln_v_g, ln_v_b,
              w_s, b_s, g_q, w_uq, g_kv, w_uk, w_uv, w_o, ln1_g, ln1_b, w_up, w_conv, b_conv,
              w_down, ln2_g, ln2_b):
    hp, hs = x_prompt, x_sample
    ckv_p, kpe_p, conv_p = [], [], []
    ckv_s, kpe_s, v_s, conv_s = [], [], [], []
    for l in range(DEPTH):
        lw = (w_in[l], ln_v_g[l], ln_v_b[l], w_s[l], b_s[l], g_q[l], w_uq[l], g_kv[l], w_uk[l],
              w_uv[l], w_o[l], ln1_g[l], ln1_b[l], w_up[l], w_conv[l], b_conv[l], w_down[l],
              ln2_g[l], ln2_b[l])
        hp, ckv_new, kpe_new, _, conv_new = trunk_layer(hp, None, None, None, *lw)
        ckv_p.append(ckv_new)
        kpe_p.append(kpe_new)
        conv_p.append(conv_new)
        hs, ckv_new, kpe_new, v_new, conv_new = trunk_layer(hs, cache_ckv[l], cache_kpe[l],
                                                            state_ffn_conv[l], *lw)
        ckv_s.append(ckv_new)
        kpe_s.append(kpe_new)
        v_s.append(v_new)
        conv_s.append(conv_new)
    return (hp, hs, jnp.stack(ckv_p), jnp.stack(kpe_p), jnp.stack(conv_p),
            jnp.stack(ckv_s), jnp.stack(kpe_s), jnp.stack(v_s), jnp.stack(conv_s))
```

```python
import math
import contextlib
import numpy as np
import concourse.bass as bass
import concourse.mybir as mybir
from concourse.bass_utils import run_bass_kernel_spmd

F32 = mybir.dt.float32
BF16 = mybir.dt.bfloat16
AF = mybir.ActivationFunctionType
ALU = mybir.AluOpType
AX = mybir.AxisListType

D = 1024
GW = 512
QL = 384
KVR = 256
ROPE = 64
DFF = 2816
NFC = 44
DEC = 16
ALPHA = 2.0 ** 0.25
ATT_SCALE = 1.0 / math.sqrt(192.0)
LN_EPS = 1e-5
RMS_EPS = 1e-6
UNIT = 2048
NUNITS = 4 + 4 + 4 + 22 + 11
U_TM, U_FM, U_WO, U_UP, U_DN = 0, 4, 8, 12, 34


class Eng:
    def __init__(self, name, sem):
        self.name, self.sem, self.count, self.seen, self.prog = name, sem, 0, {}, []
        self.abs = []


class Chan:
    def __init__(self, sem):
        self.sem, self.count = sem, 0


class Dep:
    __slots__ = ("w", "r")

    def __init__(self):
        self.w, self.r = None, {}


class K:
    def __init__(self, nc, stack):
        self.nc, self.stack = nc, stack
        self.eng = {}
        for n in ("pe", "act", "dve", "pool", "sp"):
            self.eng[n] = Eng(n, stack.enter_context(nc.semaphore("e_" + n)))

    def chan(self, name):
        return Chan(self.stack.enter_context(self.nc.semaphore("c_" + name)))

    def _waits(self, e, rd, wr):
        need = {}
        for d in rd:
            if d.w is not None:
                need[d.w[0]] = max(need.get(d.w[0], 0), d.w[1])
        for d in wr:
            if d.w is not None:
                need[d.w[0]] = max(need.get(d.w[0], 0), d.w[1])
            for o, c in d.r.items():
                need[o] = max(need.get(o, 0), c)
        for o, c in need.items():
            if o is e and e.name == "pe":
                continue
            if e.seen.get(o, 0) >= c:
                continue
            e.seen[o] = c
            e.prog.append(lambda h, s=o.sem, v=c: h.wait_ge(s, v))
            e.abs.append(("wait", o, c))

    @staticmethod
    def _mark(me, rd, wr):
        for d in rd:
            d.r[me[0]] = max(d.r.get(me[0], 0), me[1])
        for d in wr:
            d.w, d.r = me, {}

    def op(self, en, fn, rd=(), wr=(), inc=True):
        e = self.eng[en]
        self._waits(e, rd, wr)
        if inc:
            e.count += 1
            e.prog.append(lambda h, f=fn, s=e.sem: f(h).then_inc(s, 1))
            e.abs.append(("inc", e, 1))
            me = (e, e.count)
        else:
            e.prog.append(lambda h, f=fn: f(h))
            me = (e, e.count + 1)
        self._mark(me, rd, wr)

    def dma(self, qn, out, in_, ch, rd=(), wr=(), slow=False):
        e = self.eng[qn]
        self._waits(e, rd, wr)
        ch.count += 16
        if slow:
            e.prog.append(lambda h, o=out, i=in_, s=ch.sem: h.dma_start(
                out=o, in_=i, allow_slow_non_contiguous=True).then_inc(s, 16))
        else:
            e.prog.append(lambda h, o=out, i=in_, s=ch.sem: h.dma_start(out=o, in_=i).then_inc(s, 16))
        e.abs.append(("inc", ch, 16))
        self._mark((ch, ch.count), rd, wr)

    def wait_all(self, qn, chans):
        e = self.eng[qn]
        for ch in chans:
            if ch.count:
                e.prog.append(lambda h, s=ch.sem, v=ch.count: h.wait_ge(s, v))

    def check_deadlock(self):
        val = {}
        pc = {n: 0 for n in self.eng}
        progress = True
        while progress:
            progress = False
            for n, e in self.eng.items():
                while pc[n] < len(e.abs):
                    kind, o, v = e.abs[pc[n]]
                    if kind == "wait":
                        if val.get(o, 0) < v:
                            break
                    else:
                        val[o] = val.get(o, 0) + v
                    pc[n] += 1
                    progress = True
        bad = False
        for n, e in self.eng.items():
            if pc[n] < len(e.abs):
                kind, o, v = e.abs[pc[n]]
                nm = o.name if isinstance(o, Eng) else "chan"
                print("DEADLOCK: engine", n, "stuck at", pc[n], "/", len(e.abs), kind, nm, v, "have", val.get(o, 0))
                bad = True
        return not bad

    def emit(self):
        with self.nc.Block() as block:
            def mk(en):
                def run(h):
                    for f in self.eng[en].prog:
                        f(h)
                return run
            block.tensor(mk("pe"))
            block.scalar(mk("act"))
            block.vector(mk("dve"))
            block.gpsimd(mk("pool"))
            block.sync(mk("sp"))


class _Stop(Exception):
    pass


STOP = None
import os
DBG_SKIP = bool(os.environ.get('DBG_SKIP'))


TILE_NO = [0]


def _chk(stage):
    if STOP == stage or STOP == "%s@%d" % (stage, TILE_NO[0]):
        raise _Stop()


class B:
    def __init__(self, t, d=None):
        self.t, self.d = t, (d if d is not None else Dep())
        self.wd = [self.d]


def build(S, PAST, TT=256, NRING=4, NBIG=4):
    NB = TT // 128
    NT = S // TT
    NKB = max(S, PAST + 128) // 128
    nc = bass.Bass("TRN2", target_bir_lowering=False)
    stack = contextlib.ExitStack()
    with stack:
        k = K(nc, stack)

        def din(name, shape, dt=F32):
            return nc.dram_tensor(name, list(shape), dt, kind="ExternalInput").ap()

        def dout(name, shape):
            return nc.dram_tensor(name, list(shape), F32, kind="ExternalOutput").ap()

        x_d = din("x", [S, D])
        xs_d = din("xs", [DEC, D])
        cckv_d = din("cckv", [PAST, KVR])
        ckpe_d = din("ckpe", [PAST, ROPE])
        sconv_d = din("sconv", [2, 2 * DFF])
        wstream_d = din("wstream", [NUNITS, 128, UNIT])
        wuqn_d = din("wuqn", [128, 3 * 4 * 128])
        wuqpe_d = din("wuqpe", [128, 3 * 4 * 64])
        wuqsw_d = din("wuqsw", [128, 3 * 4 * 64])
        wukT_d = din("wukT", [128, 4 * 256])
        wuv_d = din("wuv", [128, 2 * 4 * 128])
        wsT_d = din("wsT", [128, 4 * 128])
        ident_d = din("ident", [128, 128])
        gq_d = din("gq", [128, 3])
        wconv_d = din("wconv", [128, NFC * 3])
        bconv_d = din("bconv", [128, NFC])
        gkv_d = din("gkv", [128, KVR])
        lnvg_d = din("lnvg", [128, GW])
        lnvb_d = din("lnvb", [128, GW])
        ln1g_d = din("ln1g", [128, D])
        ln1b_d = din("ln1b", [128, D])
        ln2g_d = din("ln2g", [128, D])
        ln2b_d = din("ln2b", [128, D])
        bsb_d = din("bsb", [128, 4 * 128])
        rtok_d = din("rtok", [NT, 128, NB * 2 * 64])
        rfm_d = din("rfm", [NT, 64, 2 * TT])
        rtok_s_d = din("rtok_s", [DEC, 2 * 64])
        rfm_s_d = din("rfm_s", [64, 2 * DEC])
        wsc_d = nc.dram_tensor("wsc", [NUNITS, 128, UNIT], BF16, kind="Internal").ap()

        y_d = dout("y", [S, D])
        ys_d = dout("ys", [DEC, D])
        ockv_d = dout("ockv", [S, KVR])
        okpe_d = dout("okpe", [S, ROPE])
        oconv_d = dout("oconv", [2, 2 * DFF])
        ockvs_d = dout("ockvs", [DEC, KVR])
        okpes_d = dout("okpes", [DEC, ROPE])
        ovs_d = dout("ovs", [DEC, GW])
        oconvs_d = dout("oconvs", [2, 2 * DFF])

        def sb(name, shape, dt):
            return B(stack.enter_context(nc.sbuf_tensor("s_" + name, list(shape), dt)))

        KT = stack.enter_context(nc.sbuf_tensor("KT", [128, 3, NKB * 128], BF16))
        Vt = stack.enter_context(nc.sbuf_tensor("Vt", [128, NKB, KVR], BF16))
        kvd = [Dep() for _ in range(NKB)]
        ring = [sb("ring%d" % i, [128, UNIT], BF16) for i in range(NRING)]
        ringch = [k.chan("ring%d" % i) for i in range(NRING)]
        ringsw = [k.chan("ringsw%d" % i) for i in range(NRING)]
        big = [sb("big%d" % i, [128, D], F32) for i in range(NBIG)]
        bigch = [k.chan("big%d" % i) for i in range(NBIG)]
        bigst = [k.chan("bigst%d" % i) for i in range(NBIG)]
        wuqn = sb("wuqn", [128, 3, 4, 128], BF16)
        wuqpe = sb("wuqpe", [128, 3, 4, 64], BF16)
        wuqsw = sb("wuqsw", [128, 3, 4, 64], BF16)
        wukT = sb("wukT", [128, 4, 256], BF16)
        wuv = sb("wuv", [128, 2, 4, 128], BF16)
        wsT = sb("wsT", [128, 4, 128], BF16)
        ident = sb("ident", [128, 128], F32)
        onesb = sb("onesb", [128, 128], BF16)
        onesf = sb("onesf", [128, 128], F32)
        epsb = sb("epsb", [128, 2], F32)
        gq = sb("gq", [128, 3], F32)
        wconv = sb("wconv", [128, NFC, 3], F32)
        bconv = sb("bconv", [128, NFC], F32)
        gkv = sb("gkv", [128, KVR], F32)
        lnvg = sb("lnvg", [128, GW], F32)
        lnvb = sb("lnvb", [128, GW], F32)
        ln1g = sb("ln1g", [128, D], F32)
        ln1b = sb("ln1b", [128, D], F32)
        ln2g = sb("ln2g", [128, D], F32)
        ln2b = sb("ln2b", [128, D], F32)
        bsb = sb("bsb", [128, 4, 128], F32)
        rtok = sb("rtok", [128, NB, 2, 64], F32)
        rfm = sb("rfm", [64, 2, TT], F32)
        rtokch, rfmch = k.chan("rtok"), k.chan("rfm")
        xT = sb("xT", [128, 8, TT], BF16)
        vn = [sb("vn%d" % i, [128, GW], BF16) for i in range(NB)]
        kvf = [sb("kvf%d" % i, [128, 320], F32) for i in range(2)]
        kvfch = [k.chan("kvf%d" % i) for i in range(2)]
        kvfst = [k.chan("kvfst%d" % i) for i in range(2)]
        st = [sb("st%d" % i, [128, 8], F32) for i in range(4)]
        junk = sb("junk", [128, D], BF16)
        junkps = [sb("junkp%d" % i, [128, 64], F32) for i in range(2)]
        uT = sb("uT", [128, 4, TT], BF16)
        sq = [sb("sq%d" % i, [128, TT], F32) for i in range(3)]
        cqg = sb("cqg", [128, 3, TT], BF16)
        rb = sb("rb", [128, TT], F32)
        crb = sb("crb", [64, 2, TT], F32)
        assert TT == 256
        Rg = stack.enter_context(nc.sbuf_tensor("s_Rg", [128, 22 * TT], BF16))
        actT = B(Rg[:, :].rearrange("p (a b) -> p a b", a=22))
        qlT = B(Rg[:, 0:2048].rearrange("p (a b c) -> p a b c", a=2, b=4))
        qnT = B(Rg[:, 2048:3072].rearrange("p (a b) -> p a b", a=4))
        qpT = B(Rg[0:64, 3072:4096].rearrange("p (a b) -> p a b", a=4))
        pT = [B(Rg[:, 4096 + 512 * i:4096 + 512 * (i + 1)]) for i in range(3)]
        actc = [Dep() for _ in range(22)]
        for b_ in [qlT, qnT, qpT] + pT:
            b_.wd = [b_.d] + actc
        alias_w = [qlT.d, qnT.d, qpT.d] + [p_.d for p_ in pT]
        mixT = xT
        rinv = sb("rinv", [128, 512], F32)
        rcp = sb("rcp", [128, 512], F32)
        gtmp = B(rinv.t[:, :].rearrange("p (a b) -> p a b", a=4), rinv.d)
        onT = sb("onT", [128, 2, 512], BF16)
        hbuf = [sb("hbuf%d" % i, [128, D], F32) for i in range(NB)]
        hT = xT
        upb = [sb("upb%d" % i, [128, TT + 2], F32) for i in range(3)]
        NCV = 2
        cvas = [sb("cva%d" % i, [128, TT], F32) for i in range(NCV)]
        cvgs = [sb("cvg%d" % i, [128, TT], F32) for i in range(NCV)]
        sils = [sb("sil%d" % i, [128, TT], F32) for i in range(NCV)]
        rp1 = B(cvas[0].t[0:64, :], cvas[0].d)
        rp2 = B(cvgs[0].t[0:64, :], cvgs[0].d)
        prev2 = sb("prev2", [128, NFC, 2], F32)
        prevch = k.chan("prev2")
        prevst = k.chan("prev2st")
        ps = [B(stack.enter_context(nc.psum_tensor("ps%d" % i, [128, 512], F32))) for i in range(8)]
        setup_ch = k.chan("setup")
        setup_sw = k.chan("setupsw")
        sc_dep = [Dep() for _ in range(NUNITS)]

        state = {"rot": 0, "big": 0, "pt": 0, "upb": 0, "st": 0, "kvf": 0,
                 "unit_issued": 0, "unit_pos": 0, "qblk": 0, "srot": 0, "mrot": 0}

        def rot():
            b = ps[4 + state["rot"] % 4]
            state["rot"] += 1
            return b

        def nxt(lst, key):
            i = state[key] % len(lst)
            state[key] += 1
            return i

        setup_deps = []

        def setup_load(buf, src, cast=False):
            k.dma("pool" if cast else "sp", buf.t[:], src, setup_sw if cast else setup_ch, wr=[buf.d])
            setup_deps.append((buf.d, cast))

        setup_load(wuqn, wuqn_d.rearrange("p (a b c) -> p a b c", a=3, b=4), cast=True)
        setup_load(wuqpe, wuqpe_d.rearrange("p (a b c) -> p a b c", a=3, b=4), cast=True)
        setup_load(wuqsw, wuqsw_d.rearrange("p (a b c) -> p a b c", a=3, b=4), cast=True)
        setup_load(wukT, wukT_d.rearrange("p (a b) -> p a b", a=4), cast=True)
        setup_load(wuv, wuv_d.rearrange("p (a b c) -> p a b c", a=2, b=4), cast=True)
        setup_load(wsT, wsT_d.rearrange("p (a b) -> p a b", a=4), cast=True)
        setup_load(ident, ident_d)
        setup_load(gq, gq_d)
        setup_load(wconv, wconv_d.rearrange("p (a b) -> p a b", b=3))
        setup_load(bconv, bconv_d)
        setup_load(gkv, gkv_d)
        setup_load(lnvg, lnvg_d)
        setup_load(lnvb, lnvb_d)
        setup_load(ln1g, ln1g_d)
        setup_load(ln1b, ln1b_d)
        setup_load(ln2g, ln2g_d)
        setup_load(ln2b, ln2b_d)
        setup_load(bsb, bsb_d.rearrange("p (a b) -> p a b", a=4))
        for d, cast in setup_deps:
            d.w = (setup_sw, setup_sw.count) if cast else (setup_ch, setup_ch.count)
        k.op("pool", lambda h: h.memset(onesb.t[:], 1.0), wr=[onesb.d])
        k.op("pool", lambda h: h.memset(onesf.t[:], 1.0), wr=[onesf.d])
        k.op("pool", lambda h: h.memset(epsb.t[:, 0:1], LN_EPS), wr=[epsb.d])
        k.op("pool", lambda h: h.memset(epsb.t[:, 1:2], RMS_EPS), wr=[epsb.d])
        k.op("pool", lambda h: h.memset(wsT.t[64:128, :, 0:64], 0.0), wr=[wsT.d])
        for u in range(NUNITS):
            s = u % NRING
            k.dma("pool", ring[s].t[:], wstream_d[u], ringsw[s], wr=[ring[s].d])
            k.dma("sp", wsc_d[u], ring[s].t[:], ringch[s], rd=[ring[s].d], wr=[sc_dep[u]])

        total_units = (NT + 1) * NUNITS

        def prefetch_units(upto):
            while state["unit_issued"] < min(upto, total_units):
                pos = state["unit_issued"]
                u = pos % NUNITS
                s = pos % NRING
                k.dma("sp", ring[s].t[:], wsc_d[u], ringch[s], rd=[sc_dep[u]], wr=[ring[s].d])
                state["unit_issued"] += 1

        def get_unit(u):
            pos = state["unit_pos"]
            assert pos % NUNITS == u, (pos, u)
            prefetch_units(pos + NRING)
            state["unit_pos"] += 1
            return ring[pos % NRING]

        def ln_multi(items, width, gam, bet):
            for _ in ln_multi_gen(items, width, gam, bet):
                pass

        def ln_multi_gen(items, width, gam, bet):
            ctx = []
            for (src, n, out_ap, out_dep) in items:
                ctx.append((src, n, out_ap, out_dep, st[nxt(st, "st")], src.t[0:n, 0:width]))
            for (src, n, out_ap, out_dep, s_, xs) in ctx:
                k.op("dve", lambda h, s_=s_, xs=xs, n=n: h.reduce_sum(s_.t[0:n, 0:1], xs, axis=AX.X),
                     rd=[src.d], wr=[s_.d])
            yield
            for (src, n, out_ap, out_dep, s_, xs) in ctx:
                k.op("dve", lambda h, s_=s_, n=n: h.tensor_scalar(s_.t[0:n, 1:2], s_.t[0:n, 0:1], -1.0 / width, None,
                                                                  op0=ALU.mult), rd=[s_.d], wr=[s_.d])
            yield
            for (src, n, out_ap, out_dep, s_, xs) in ctx:
                k.op("act", lambda h, s_=s_, xs=xs, n=n: h.activation(
                    junk.t[0:n, 0:width], xs, AF.Square, bias=s_.t[0:n, 1:2], accum_out=s_.t[0:n, 2:3]),
                    rd=[src.d, s_.d], wr=[junk.d, s_.d])
            yield
            for (src, n, out_ap, out_dep, s_, xs) in ctx:
                k.op("act", lambda h, s_=s_, n=n: h.activation(
                    s_.t[0:n, 3:4], s_.t[0:n, 2:3], AF.Sqrt, bias=epsb.t[0:n, 0:1], scale=1.0 / width),
                    rd=[s_.d, epsb.d], wr=[s_.d])
            yield
            for (src, n, out_ap, out_dep, s_, xs) in ctx:
                k.op("dve", lambda h, s_=s_, n=n: h.reciprocal(s_.t[0:n, 4:5], s_.t[0:n, 3:4]), rd=[s_.d], wr=[s_.d])
            yield
            for (src, n, out_ap, out_dep, s_, xs) in ctx:
                k.op("dve", lambda h, s_=s_, xs=xs, n=n: h.scalar_tensor_tensor(
                    xs, xs, s_.t[0:n, 1:2], gam.t[0:n, 0:width], op0=ALU.add, op1=ALU.mult),
                    rd=[src.d, s_.d, gam.d], wr=[src.d])
            yield
            for (src, n, out_ap, out_dep, s_, xs) in ctx:
                wr = [src.d] if out_dep is src.d else [out_dep]
                k.op("dve", lambda h, s_=s_, xs=xs, n=n, out_ap=out_ap: h.scalar_tensor_tensor(
                    out_ap, xs, s_.t[0:n, 4:5], bet.t[0:n, 0:width], op0=ALU.mult, op1=ALU.add),
                    rd=[src.d, s_.d, bet.d], wr=wr)
            yield

        def ln_rows(src, n, width, gam, bet, out_ap, out_dep):
            ln_multi([(src, n, out_ap, out_dep)], width, gam, bet)

        def transpose_to(dst, dst_dep, src, src_dep, n, nchunks):
            c = 0
            while c < nchunks:
                g = min(4, nchunks - c)
                pb = rot()
                for i in range(g):
                    k.op("pe", lambda h, i=i, c=c, pb=pb: h.transpose(
                        pb.t[:, i * 128:i * 128 + n], src(c + i), ident.t[0:n, 0:n]),
                        rd=[src_dep, ident.d], wr=[pb.d], inc=(i == g - 1))
                srcv = pb.t[:, 0:g * 128].rearrange("p (a b) -> p a b", a=g)[:, :, 0:n]
                k.op("act", lambda h, c=c, g=g, srcv=srcv: h.copy(dst(c, g), srcv), rd=[pb.d], wr=[dst_dep])
                c += g

        def append_kv(kb, kf, n):
            k.op("pool", lambda h: h.tensor_copy(Vt[0:n, kb, :], kf.t[0:n, 0:KVR]), rd=[kf.d], wr=[kvd[kb]])
            pb = rot()
            for i in range(3):
                w_ = 128 if i < 2 else 64
                k.op("pe", lambda h, i=i, w_=w_, pb=pb: h.transpose(
                    pb.t[0:w_, i * 128:i * 128 + n], kf.t[0:n, i * 128:i * 128 + w_], ident.t[0:n, 0:n]),
                    rd=[kf.d, ident.d], wr=[pb.d], inc=(i == 2))
            k.op("act", lambda h, pb=pb: h.copy(
                KT[:, 0:2, kb * 128:kb * 128 + n],
                pb.t[:, 0:256].rearrange("p (a b) -> p a b", a=2)[:, :, 0:n]), rd=[pb.d], wr=[kvd[kb]])
            k.op("act", lambda h, pb=pb: h.copy(KT[0:64, 2, kb * 128:kb * 128 + n], pb.t[0:64, 256:256 + n]),
                 rd=[pb.d], wr=[kvd[kb]])

        def load_x_blocks(src_d, row0, blocks):
            res = []
            for (o, n) in blocks:
                bi = nxt(big, "big")
                k.dma("sp", big[bi].t[0:n, :], src_d[row0 + o:row0 + o + n, :], bigch[bi], wr=[big[bi].d])
                res.append(bi)
            return res

        def tile(sample, t, xa_idx, next_x, b_done=False):
            n = DEC if sample else TT
            blocks = [(0, DEC)] if sample else [(b * 128, 128) for b in range(NB)]
            src_d = xs_d if sample else x_d
            row0 = 0 if sample else t * TT
            kb0 = PAST // 128 if sample else t * NB
            y_out = ys_d if sample else y_d
            ckv_out = ockvs_d if sample else ockv_d
            kpe_out = okpes_d if sample else okpe_d

            if sample:
                k.dma("sp", rtok.t[0:DEC, 0, :, :], rtok_s_d.rearrange("p (a b) -> p a b", a=2), rtokch, wr=[rtok.d])
                k.dma("sp", rfm.t[:, :, 0:DEC], rfm_s_d.rearrange("p (a b) -> p a b", a=2), rfmch, wr=[rfm.d])
            else:
                k.dma("sp", rtok.t[:], rtok_d[t].rearrange("p (a b c) -> p a b c", a=NB, b=2), rtokch, wr=[rtok.d])
                k.dma("sp", rfm.t[:], rfm_d[t].rearrange("p (a b) -> p a b", a=2), rfmch, wr=[rfm.d])

            def stage_b(idx_list, blks):
                for bi_, (o, nb_) in zip(idx_list, blks):
                    xb_ = big[bi_]
                    transpose_to(lambda c, g, o=o, nb_=nb_: xT.t[:, c:c + g, o:o + nb_], xT.d,
                                 lambda c, xb_=xb_, nb_=nb_: xb_.t[0:nb_, c * 128:(c + 1) * 128], xb_.d, nb_, 8)
            if not b_done:
                stage_b(xa_idx, blocks)

            _chk("B")
            accs = [ps[i] for i in range(4)]
            for uu in range(4):
                un = get_unit(U_TM + uu)
                w3 = un.t[:, 0:2 * 832].rearrange("p (a b) -> p a b", a=2)
                for bi, (o, nb_) in enumerate(blocks):
                    for kk in range(2):
                        kc = uu * 2 + kk
                        for g, (c0, c1) in enumerate(((0, 512), (512, 832))):
                            a_ = accs[bi * 2 + g]
                            k.op("pe", lambda h, a_=a_, kc=kc, o=o, nb_=nb_, kk=kk, c0=c0, c1=c1, w3=w3: h.matmul(
                                a_.t[0:nb_, 0:c1 - c0], xT.t[:, kc, o:o + nb_], w3[:, kk, c0:c1],
                                start=(kc == 0), stop=(kc == 7)), rd=[xT.d, un.d], wr=[a_.d],
                                inc=(kc == 7 or (bi == len(blocks) - 1 and kk == 1 and g == 1)))

            _chk("C1")
            sqs = []
            for uu in range(4):
                un = get_unit(U_FM + uu)
                w4 = un.t[:].rearrange("p (m a b) -> p m a b", m=2, a=8)
                for mm in range(2):
                    ci = uu * 2 + mm
                    if ci >= 7:
                        continue
                    pb = rot()
                    for kc in range(8):
                        k.op("pe", lambda h, pb=pb, kc=kc, mm=mm, w4=w4: h.matmul(
                            pb.t[:, 0:n], w4[:, mm, kc, :], xT.t[:, kc, 0:n], start=(kc == 0), stop=(kc == 7)),
                            rd=[xT.d, un.d], wr=[pb.d], inc=(kc == 7))
                    if ci == 0:
                        _chk("C2mm")
                    if ci < 4:
                        k.op("act", lambda h, pb=pb, ci=ci: h.activation(uT.t[:, ci, 0:n], pb.t[:, 0:n],
                                                                         AF.Gelu_apprx_tanh), rd=[pb.d], wr=[uT.d])
                        if ci == 0:
                            _chk("C2a")
                        if ci == 3:
                            _chk("C2u")
                    else:
                        cc = ci - 4
                        sqb = sq[cc]
                        if cc == 0:
                            _chk("C2q0mm")
                        k.op("act", lambda h, pb=pb, sqb=sqb: h.activation(sqb.t[:, 0:n], pb.t[:, 0:n], AF.Square),
                             rd=[pb.d], wr=[sqb.d])
                        if cc == 0:
                            _chk("C2q0sq")
                        k.op("act", lambda h, pb=pb, cc=cc: h.activation(
                            cqg.t[:, cc, 0:n], pb.t[:, 0:n], AF.Copy, scale=gq.t[:, cc:cc + 1]),
                            rd=[pb.d, gq.d], wr=[cqg.d])
                        sqs.append(sqb)
                        if cc == 0:
                            _chk("C2q0")
                        if cc == 1:
                            _chk("C2q1")

            _chk("C2")
            kfs, sts, vts = [], [], []
            for bi, (o, nb_) in enumerate(blocks):
                ki = nxt(kvf, "kvf")
                kfs.append(ki)
                sts.append(st[nxt(st, "st")])
            for bi, (o, nb_) in enumerate(blocks):
                akv, s_ = accs[bi * 2 + 1], sts[bi]
                k.op("act", lambda h, akv=akv, s_=s_, nb_=nb_: h.activation(
                    junk.t[0:nb_, 0:KVR], akv.t[0:nb_, 0:KVR], AF.Square, accum_out=s_.t[0:nb_, 0:1]),
                    rd=[akv.d], wr=[junk.d, s_.d])
            for bi, (o, nb_) in enumerate(blocks):
                s_ = sts[bi]
                k.op("act", lambda h, s_=s_, nb_=nb_: h.activation(
                    s_.t[0:nb_, 1:2], s_.t[0:nb_, 0:1], AF.Sqrt, bias=epsb.t[0:nb_, 1:2], scale=1.0 / KVR),
                    rd=[s_.d, epsb.d], wr=[s_.d])
            for bi, (o, nb_) in enumerate(blocks):
                akv, kf, jp = accs[bi * 2 + 1], kvf[kfs[bi]], junkps[bi % 2]
                k.op("dve", lambda h, kf=kf, akv=akv, bi=bi, nb_=nb_: h.tensor_tensor(
                    kf.t[0:nb_, 256:320], akv.t[0:nb_, 256:320], rtok.t[0:nb_, bi, 0, :], op=ALU.mult),
                    rd=[akv.d, rtok.d], wr=[kf.d])
                k.op("dve", lambda h, akv=akv, bi=bi, nb_=nb_, jp=jp: h.tensor_tensor(
                    jp.t[0:nb_, 0:32], akv.t[0:nb_, 288:320], rtok.t[0:nb_, bi, 1, 0:32], op=ALU.mult),
                    rd=[akv.d, rtok.d], wr=[jp.d])
                k.op("dve", lambda h, akv=akv, bi=bi, nb_=nb_, jp=jp: h.tensor_tensor(
                    jp.t[0:nb_, 32:64], akv.t[0:nb_, 256:288], rtok.t[0:nb_, bi, 1, 32:64], op=ALU.mult),
                    rd=[akv.d, rtok.d], wr=[jp.d])
                k.op("dve", lambda h, kf=kf, nb_=nb_, jp=jp: h.tensor_tensor(
                    kf.t[0:nb_, 256:320], kf.t[0:nb_, 256:320], jp.t[0:nb_, :], op=ALU.add),
                    rd=[kf.d, jp.d], wr=[kf.d])
            for bi, (o, nb_) in enumerate(blocks):
                s_ = sts[bi]
                k.op("dve", lambda h, s_=s_, nb_=nb_: h.reciprocal(s_.t[0:nb_, 2:3], s_.t[0:nb_, 1:2]),
                     rd=[s_.d], wr=[s_.d])
            for bi, (o, nb_) in enumerate(blocks):
                akv, kf, s_ = accs[bi * 2 + 1], kvf[kfs[bi]], sts[bi]
                k.op("dve", lambda h, kf=kf, akv=akv, s_=s_, nb_=nb_: h.scalar_tensor_tensor(
                    kf.t[0:nb_, 0:KVR], akv.t[0:nb_, 0:KVR], s_.t[0:nb_, 2:3], gkv.t[0:nb_, :],
                    op0=ALU.mult, op1=ALU.mult), rd=[akv.d, s_.d, gkv.d], wr=[kf.d])
            for bi, (o, nb_) in enumerate(blocks):
                av = accs[bi * 2]
                vi = nxt(big, "big")
                vts.append(vi)
                vt = big[vi]
                k.op("act", lambda h, vt=vt, av=av, nb_=nb_: h.activation(vt.t[0:nb_, 0:GW], av.t[0:nb_, :],
                                                                          AF.Gelu_apprx_tanh), rd=[av.d], wr=[vt.d])
            for bi, (o, nb_) in enumerate(blocks):
                kf, ki = kvf[kfs[bi]], kfs[bi]
                r0 = row0 + o
                k.dma("pool", ckv_out[r0:r0 + nb_, :], kf.t[0:nb_, 0:KVR], kvfst[ki], rd=[kf.d])
                k.dma("pool", kpe_out[r0:r0 + nb_, :], kf.t[0:nb_, 256:320], kvfst[ki], rd=[kf.d])
                append_kv(kb0 + bi, kf, nb_)
            def c3b_gen():
                yield from ln_multi_gen([(big[vts[bi]], nb_, big[vts[bi]].t[0:nb_, 0:GW], big[vts[bi]].d)
                                         for bi, (o, nb_) in enumerate(blocks)], GW, lnvg, lnvb)
                for bi, (o, nb_) in enumerate(blocks):
                    vt = big[vts[bi]]
                    k.op("pool", lambda h, vt=vt, bi=bi, nb_=nb_: h.tensor_copy(vn[bi].t[0:nb_, :], vt.t[0:nb_, 0:GW]),
                         rd=[vt.d], wr=[vn[bi].d])
                    if sample:
                        k.dma("pool", ovs_d[0:nb_, :], vt.t[0:nb_, 0:GW], bigst[vts[bi]], rd=[vt.d])

            _chk("C3")
            def stage_d():
                for bi, (o, nb_) in enumerate(blocks):
                    pb = rot()
                    for hh in range(4):
                        k.op("pe", lambda h, pb=pb, hh=hh, bi=bi, nb_=nb_: h.matmul(
                            pb.t[:, hh * 128:hh * 128 + nb_], vn[bi].t[0:nb_, hh * 128:(hh + 1) * 128],
                            wsT.t[0:nb_, hh, 0:nb_], start=True, stop=True),
                            rd=[vn[bi].d, wsT.d], wr=[pb.d], inc=(hh == 3))
                    k.op("dve", lambda h, pb=pb, nb_=nb_: h.tensor_tensor(
                        gtmp.t[:, :, 0:nb_], pb.t[:, 0:512].rearrange("p (a b) -> p a b", a=4)[:, :, 0:nb_],
                        bsb.t[:, :, 0:nb_], op=ALU.add), rd=[pb.d, bsb.d], wr=[gtmp.d])
                    k.op("dve", lambda h, o=o, nb_=nb_: h.tensor_tensor(
                        mixT.t[:, 0:4, o:o + nb_], gtmp.t[:, :, 0:nb_], uT.t[:, :, o:o + nb_], op=ALU.mult),
                        rd=[gtmp.d, uT.d], wr=[mixT.d])

            _chk("D")
            rs = rot()
            for cc in range(3):
                k.op("pe", lambda h, cc=cc, rs=rs: h.matmul(rs.t[:, 0:n], onesf.t[:], sq[cc].t[:, 0:n],
                                                            start=(cc == 0), stop=(cc == 2)),
                     rd=[sq[cc].d, onesf.d], wr=[rs.d], inc=(cc == 2))
            k.op("act", lambda h, rs=rs: h.activation(rb.t[:, 0:n], rs.t[:, 0:n], AF.Sqrt, bias=epsb.t[:, 1:2],
                                                      scale=1.0 / QL), rd=[rs.d, epsb.d], wr=[rb.d])
            k.op("dve", lambda h: h.reciprocal(rb.t[:, 0:n], rb.t[:, 0:n]), rd=[rb.d], wr=[rb.d])
            for tb in range(2):
                k.op("dve", lambda h, tb=tb: h.tensor_tensor(
                    crb.t[:, tb, 0:n], rfm.t[:, tb, 0:n], rb.t[0:64, 0:n], op=ALU.mult),
                    rd=[rfm.d, rb.d], wr=[crb.d])
            for hh in range(4):
                pb = rot()
                for rk in range(3):
                    k.op("pe", lambda h, pb=pb, rk=rk, hh=hh: h.matmul(
                        pb.t[:, 0:n], wuqn.t[:, rk, hh, :], cqg.t[:, rk, 0:n], start=(rk == 0), stop=(rk == 2)),
                        rd=[wuqn.d, cqg.d], wr=[pb.d], inc=(rk == 2))
                k.op("dve", lambda h, pb=pb, hh=hh: h.tensor_tensor(qnT.t[:, hh, 0:n], pb.t[:, 0:n], rb.t[:, 0:n],
                                                                    op=ALU.mult), rd=[pb.d, rb.d], wr=qnT.wd)
            for hh in range(4):
                pbx, pbs = rot(), rot()
                for (pb_, w_) in ((pbx, wuqpe), (pbs, wuqsw)):
                    for rk in range(3):
                        k.op("pe", lambda h, pb_=pb_, w_=w_, rk=rk, hh=hh: h.matmul(
                            pb_.t[0:64, 0:n], w_.t[:, rk, hh, :], cqg.t[:, rk, 0:n], start=(rk == 0), stop=(rk == 2)),
                            rd=[w_.d, cqg.d], wr=[pb_.d], inc=(rk == 2))
                k.op("dve", lambda h, pbx=pbx: h.tensor_tensor(rp1.t[:, 0:n], pbx.t[0:64, 0:n], crb.t[:, 0, 0:n],
                                                               op=ALU.mult), rd=[pbx.d, crb.d], wr=[rp1.d])
                k.op("dve", lambda h, pbs=pbs: h.tensor_tensor(rp2.t[:, 0:n], pbs.t[0:64, 0:n], crb.t[:, 1, 0:n],
                                                               op=ALU.mult), rd=[pbs.d, crb.d], wr=[rp2.d])
                k.op("pool", lambda h, hh=hh: h.tensor_tensor(qpT.t[:, hh, 0:n], rp1.t[:, 0:n], rp2.t[:, 0:n],
                                                              op=ALU.add), rd=[rp1.d, rp2.d], wr=qpT.wd)
            for hh in range(4):
                for cc in range(2):
                    pb = rot()
                    k.op("pe", lambda h, pb=pb, hh=hh, cc=cc: h.matmul(
                        pb.t[:, 0:n], wukT.t[:, hh, cc * 128:(cc + 1) * 128], qnT.t[:, hh, 0:n],
                        start=True, stop=True), rd=[wukT.d, qnT.d], wr=[pb.d])
                    k.op("act", lambda h, pb=pb, hh=hh, cc=cc: h.copy(qlT.t[:, cc, hh, 0:n], pb.t[:, 0:n]),
                         rd=[pb.d], wr=qlT.wd)

            _chk("F")
            xb_idx = load_x_blocks(src_d, row0, blocks)

            steps = []
            for bi, (o, nq) in enumerate(blocks):
                own = kb0 + bi
                nkeys = [(kb, 128) for kb in range(own)] + [(own, nq)]
                qg = state["qblk"]
                state["qblk"] += 1
                for j_, (kb, kn) in enumerate(nkeys):
                    steps.append(dict(bi=bi, o=o, nq=nq, kb=kb, kn=kn, first=(j_ == 0), last=(j_ == len(nkeys) - 1),
                                      own=own, acc=[ps[c_] for c_ in range(3)]))

            def emit_scores(stp):
                o, nq, kb, kn = stp["o"], stp["nq"], stp["kb"], stp["kn"]
                W = 4 * nq
                sp_ = ps[3 + state["srot"] % 3]
                state["srot"] += 1
                spv = sp_.t[0:kn, 0:W].rearrange("p (a b) -> p a b", a=4)
                for cc in range(3):
                    if cc < 2:
                        lhs = KT[:, cc, kb * 128:kb * 128 + kn]
                        rhs = qlT.t[:, cc, :, o:o + nq]
                    else:
                        lhs = KT[0:64, 2, kb * 128:kb * 128 + kn]
                        rhs = qpT.t[:, :, o:o + nq]
                    k.op("pe", lambda h, spv=spv, lhs=lhs, rhs=rhs, cc=cc: h.matmul(
                        spv, lhs, rhs, start=(cc == 0), stop=(cc == 2)),
                        rd=[kvd[kb], qlT.d, qpT.d], wr=[sp_.d], inc=(cc == 2))
                pt = pT[nxt(pT, "pt")]
                if (not sample) and kb == stp["own"]:
                    k.op("pool", lambda h, pt=pt: h.memset(
                        pt.t[64:128, :].rearrange("p (a b) -> p a b", a=4)[:, :, 0:64], 0.0), wr=pt.wd)
                    k.op("act", lambda h, pt=pt, sp_=sp_, W=W: h.activation(
                        pt.t[0:64, 0:W], sp_.t[0:64, 0:W], AF.Exp, scale=ATT_SCALE), rd=[sp_.d], wr=pt.wd)
                    k.op("act", lambda h, pt=pt, sp_=sp_: h.activation(
                        pt.t[64:128, :].rearrange("p (a b) -> p a b", a=4)[:, :, 64:128],
                        sp_.t[64:128, :].rearrange("p (a b) -> p a b", a=4)[:, :, 64:128],
                        AF.Exp, scale=ATT_SCALE), rd=[sp_.d], wr=pt.wd)
                else:
                    k.op("act", lambda h, pt=pt, sp_=sp_, kn=kn, W=W: h.activation(
                        pt.t[0:kn, 0:W], sp_.t[0:kn, 0:W], AF.Exp, scale=ATT_SCALE), rd=[sp_.d], wr=pt.wd)
                stp["pt"] = pt

            def emit_pv(stp):
                nq, kb, kn, first, last, acc = stp["nq"], stp["kb"], stp["kn"], stp["first"], stp["last"], stp["acc"]
                W = 4 * nq
                pt = stp["pt"]
                for cc in range(2):
                    k.op("pe", lambda h, pt=pt, kb=kb, kn=kn, cc=cc, first=first, last=last, acc=acc, W=W: h.matmul(
                        acc[cc].t[:, 0:W], Vt[0:kn, kb, cc * 128:(cc + 1) * 128], pt.t[0:kn, 0:W],
                        start=first, stop=last), rd=[kvd[kb], pt.d], wr=[acc[cc].d], inc=(cc == 1))
                if first:
                    k.op("dve", lambda h, pt=pt, W=W: h.tensor_copy(rinv.t[:, 0:W], pt.t[:, 0:W]),
                         rd=[pt.d], wr=[rinv.d])
                else:
                    k.op("dve", lambda h, pt=pt, kn=kn, W=W: h.tensor_tensor(
                        rinv.t[0:kn, 0:W], rinv.t[0:kn, 0:W], pt.t[0:kn, 0:W], op=ALU.add),
                        rd=[pt.d, rinv.d], wr=[rinv.d])
                if last:
                    k.op("pe", lambda h, acc=acc, W=W: h.matmul(
                        acc[2].t[:, 0:W], onesf.t[:], rinv.t[:, 0:W], start=True, stop=True),
                        rd=[onesf.d, rinv.d], wr=[acc[2].d], inc=True)

            def emit_norm(stp):
                acc, W = stp["acc"], 4 * stp["nq"]
                for cc in range(2):
                    k.op("act", lambda h, cc=cc, acc=acc, W=W: h.copy(onT.t[:, cc, 0:W], acc[cc].t[:, 0:W]),
                         rd=[acc[cc].d], wr=[onT.d])
                k.op("dve", lambda h, acc=acc, W=W: h.reciprocal(rcp.t[:, 0:W], acc[2].t[:, 0:W]),
                     rd=[acc[2].d], wr=[rcp.d])

            def emit_mla(stp):
                o, nq = stp["o"], stp["nq"]
                W = 4 * nq
                pb = ps[6 + state["mrot"] % 2]
                state["mrot"] += 1
                for hh in range(4):
                    for cc in range(2):
                        k.op("pe", lambda h, pb=pb, hh=hh, cc=cc, nq=nq: h.matmul(
                            pb.t[:, hh * nq:(hh + 1) * nq], wuv.t[:, cc, hh, :], onT.t[:, cc, hh * nq:(hh + 1) * nq],
                            start=(cc == 0), stop=(cc == 1)), rd=[wuv.d, onT.d], wr=[pb.d],
                            inc=(hh == 3 and cc == 1))
                k.op("dve", lambda h, pb=pb, o=o, nq=nq, W=W: h.tensor_tensor(
                    mixT.t[:, 4:8, o:o + nq], pb.t[:, 0:W].rearrange("p (a b) -> p a b", a=4),
                    rcp.t[:, 0:W].rearrange("p (a b) -> p a b", a=4), op=ALU.mult),
                    rd=[pb.d, rcp.d], wr=[mixT.d])

            c3b_it = [None]
            pending = None
            SKEW = 2
            for si in range(len(steps) + SKEW):
                if si < len(steps):
                    emit_scores(steps[si])
                if si >= SKEW:
                    prev = steps[si - SKEW]
                    emit_pv(prev)
                    if pending is not None:
                        pending[1] -= 1
                        if pending[1] <= 0:
                            emit_mla(pending[0])
                            pending = None
                    if prev["last"]:
                        if pending is not None:
                            emit_mla(pending[0])
                        emit_norm(prev)
                        pending = [prev, 8]
                        if c3b_it[0] is None:
                            c3b_it[0] = c3b_gen()
                    elif c3b_it[0] is not None:
                        next(c3b_it[0], None)
            if pending is not None:
                emit_mla(pending[0])
            if c3b_it[0] is None:
                c3b_it[0] = c3b_gen()
            for _ in c3b_it[0]:
                pass
            stage_d()

            _chk("G")
            for uu in range(4):
                un = get_unit(U_WO + uu)
                w3 = un.t[:].rearrange("p (a b) -> p a b", a=2)
                for bi, (o, nb_) in enumerate(blocks):
                    for kk in range(2):
                        kc = uu * 2 + kk
                        for g in range(2):
                            a_ = accs[bi * 2 + g]
                            k.op("pe", lambda h, a_=a_, kc=kc, o=o, nb_=nb_, kk=kk, g=g, w3=w3: h.matmul(
                                a_.t[0:nb_, :], mixT.t[:, kc, o:o + nb_], w3[:, kk, g * 512:(g + 1) * 512],
                                start=(kc == 0), stop=(kc == 7)), rd=[mixT.d, un.d], wr=[a_.d],
                                inc=(kc == 7 or (bi == len(blocks) - 1 and kk == 1 and g == 1)))
            for bi, (o, nb_) in enumerate(blocks):
                xb_ = big[xb_idx[bi]]
                for g in range(2):
                    a_ = accs[bi * 2 + g]
                    k.op("dve", lambda h, xb_=xb_, a_=a_, g=g, nb_=nb_: h.scalar_tensor_tensor(
                        xb_.t[0:nb_, g * 512:(g + 1) * 512], xb_.t[0:nb_, g * 512:(g + 1) * 512], ALPHA,
                        a_.t[0:nb_, :], op0=ALU.mult, op1=ALU.add), rd=[xb_.d, a_.d], wr=[xb_.d])
            ln_multi([(big[xb_idx[bi]], nb_, hbuf[bi].t[0:nb_, :], hbuf[bi].d) for bi, (o, nb_) in enumerate(blocks)],
                     D, ln1g, ln1b)
            for bi, (o, nb_) in enumerate(blocks):
                hb = hbuf[bi]
                transpose_to(lambda c, g, o=o, nb_=nb_: hT.t[:, c:c + g, o:o + nb_], hT.d,
                             lambda c, hb=hb, nb_=nb_: hb.t[0:nb_, c * 128:(c + 1) * 128], hb.d, nb_, 8)

            _chk("H")
            nxt_idx = next_x() if next_x is not None else None

            gate_pending = None
            for j in range(22):
                cva, cvg, sil = cvas[j % NCV], cvgs[j % NCV], sils[j % NCV]
                un = get_unit(U_UP + j)
                w4 = un.t[:].rearrange("p (m a b) -> p m a b", m=2, a=8)
                for half in range(2):
                    m = j + 22 * half
                    pb = rot()
                    for kc in range(8):
                        k.op("pe", lambda h, pb=pb, kc=kc, half=half, w4=w4: h.matmul(
                            pb.t[:, 0:n], w4[:, half, kc, :], hT.t[:, kc, 0:n], start=(kc == 0), stop=(kc == 7)),
                            rd=[hT.d, un.d], wr=[pb.d], inc=(kc == 7))
                    ub = upb[nxt(upb, "upb")]
                    k.op("pool", lambda h, ub=ub, m=m: h.tensor_copy(ub.t[:, 0:2], prev2.t[:, m, :]),
                         rd=[prev2.d], wr=[ub.d])
                    k.op("act", lambda h, ub=ub, pb=pb: h.copy(ub.t[:, 2:2 + n], pb.t[:, 0:n]), rd=[pb.d], wr=[ub.d])
                    k.op("pool", lambda h, ub=ub, m=m: h.tensor_copy(prev2.t[:, m, :], ub.t[:, n:n + 2]),
                         rd=[ub.d], wr=[prev2.d])
                    cv = cva if half == 0 else cvg
                    k.op("act", lambda h, pb=pb, cv=cv, m=m: h.activation(
                        cv.t[:, 0:n], pb.t[:, 0:n], AF.Identity, bias=bconv.t[:, m:m + 1], scale=wconv.t[:, m, 2:3]),
                        rd=[pb.d, wconv.d, bconv.d], wr=[cv.d])
                    for tap in (0, 1):
                        k.op("dve", lambda h, ub=ub, cv=cv, m=m, tap=tap: h.scalar_tensor_tensor(
                            cv.t[:, 0:n], ub.t[:, tap:tap + n], wconv.t[:, m, tap:tap + 1], cv.t[:, 0:n],
                            op0=ALU.mult, op1=ALU.add), rd=[ub.d, wconv.d, cv.d], wr=[cv.d])
                if gate_pending is not None:
                    gate_pending()

                def gate(j=j, sil=sil, cva=cva, cvg=cvg):
                    k.op("act", lambda h: h.activation(sil.t[:, 0:n], cva.t[:, 0:n], AF.Silu),
                         rd=[cva.d], wr=[sil.d])
                    k.op("dve", lambda h: h.tensor_tensor(
                        actT.t[:, j, 0:n], sil.t[:, 0:n], cvg.t[:, 0:n], op=ALU.mult),
                        rd=[sil.d, cvg.d], wr=[actc[j]] + alias_w)
                gate_pending = gate
            gate_pending()

            _chk("I")
            for uu in range(11):
                un = get_unit(U_DN + uu)
                w3 = un.t[:].rearrange("p (a b) -> p a b", a=2)
                for bi, (o, nb_) in enumerate(blocks):
                    for kk in range(2):
                        kc = uu * 2 + kk
                        for g in range(2):
                            a_ = accs[bi * 2 + g]
                            k.op("pe", lambda h, a_=a_, kc=kc, o=o, nb_=nb_, kk=kk, g=g, w3=w3: h.matmul(
                                a_.t[0:nb_, :], actT.t[:, kc, o:o + nb_], w3[:, kk, g * 512:(g + 1) * 512],
                                start=(kc == 0), stop=(kc == 21)), rd=[actc[kc], un.d], wr=[a_.d],
                                inc=(kc == 21 or (bi == len(blocks) - 1 and kk == 1 and g == 1)))
            if nxt_idx is not None:
                stage_b(nxt_idx, [(b * 128, 128) for b in range(NB)])
            yis = []
            for bi, (o, nb_) in enumerate(blocks):
                yi = nxt(big, "big")
                yis.append(yi)
                yb = big[yi]
                hb = hbuf[bi]
                for g in range(2):
                    a_ = accs[bi * 2 + g]
                    k.op("dve", lambda h, yb=yb, hb=hb, a_=a_, g=g, nb_=nb_: h.scalar_tensor_tensor(
                        yb.t[0:nb_, g * 512:(g + 1) * 512], hb.t[0:nb_, g * 512:(g + 1) * 512], ALPHA,
                        a_.t[0:nb_, :], op0=ALU.mult, op1=ALU.add), rd=[hb.d, a_.d], wr=[yb.d])
            ln_multi([(big[yis[bi]], nb_, big[yis[bi]].t[0:nb_, :], big[yis[bi]].d)
                      for bi, (o, nb_) in enumerate(blocks)], D, ln2g, ln2b)
            for bi, (o, nb_) in enumerate(blocks):
                r0 = row0 + o
                k.dma("pool", y_out[r0:r0 + nb_, :], big[yis[bi]].t[0:nb_, :], bigst[yis[bi]], rd=[big[yis[bi]].d])
            return nxt_idx

        def main_program():
            for part in range(4):
                for tt_ in range(2):
                    k.dma("sp", prev2.t[:, part * 11:(part + 1) * 11, tt_],
                          sconv_d[tt_, part * 1408:(part + 1) * 1408].rearrange("(m p) -> p m", p=128),
                          prevch, wr=[prev2.d], slow=True)
            for kb in range(PAST // 128):
                ki = nxt(kvf, "kvf")
                kf = kvf[ki]
                k.dma("sp", kf.t[:, 0:KVR], cckv_d[kb * 128:(kb + 1) * 128, :], kvfch[ki], wr=[kf.d])
                k.dma("sp", kf.t[:, 256:320], ckpe_d[kb * 128:(kb + 1) * 128, :], kvfch[ki], wr=[kf.d])
                append_kv(kb, kf, 128)
            _chk("cache")
            xs_idx = load_x_blocks(xs_d, 0, [(0, DEC)])
            TILE_NO[0] = 0
            x0_idx = tile(True, 0, xs_idx, lambda: load_x_blocks(x_d, 0, [(b * 128, 128) for b in range(NB)]))
            _chk("sample_done")
            for part in range(0 if DBG_SKIP else 4):
                for tt_ in range(2):
                    k.dma("pool", oconvs_d[tt_, part * 1408:(part + 1) * 1408].rearrange("(m p) -> p m", p=128),
                          prev2.t[:, part * 11:(part + 1) * 11, tt_], prevst, rd=[prev2.d], slow=True)
            k.op("pool", lambda h: h.memset(prev2.t[:], 0.0), wr=[prev2.d])
            cur = x0_idx
            for t in range(NT):
                if t + 1 < NT:
                    nx = (lambda t=t: load_x_blocks(x_d, (t + 1) * TT, [(b * 128, 128) for b in range(NB)]))
                else:
                    nx = None
                TILE_NO[0] = t + 1
                cur = tile(False, t, cur, nx, b_done=True)
                _chk("tile_done")
            for part in range(4):
                for tt_ in range(2):
                    k.dma("pool", oconv_d[tt_, part * 1408:(part + 1) * 1408].rearrange("(m p) -> p m", p=128),
                          prev2.t[:, part * 11:(part + 1) * 11, tt_], prevst, rd=[prev2.d], slow=True)
        try:
            _chk("setup")
            main_program()
        except _Stop:
            pass
        k.wait_all("pool", bigch + bigst + kvfch + kvfst + [prevch, prevst] + ringch + ringsw
                   + [setup_ch, setup_sw, rtokch, rfmch])
        assert k.check_deadlock(), "build-time deadlock check failed"
        k.emit()
    return nc


def _rope_tables(pos):
    inv = np.power(np.float32(10000.0), -np.arange(0, ROPE, 2, dtype=np.float32) / np.float32(ROPE)).astype(np.float32)
    ang = pos.astype(np.float32)[:, None] * inv[None, :]
    c, s = np.cos(ang).astype(np.float32), np.sin(ang).astype(np.float32)
    return np.concatenate([c, c], 1), np.concatenate([-s, s], 1)


def _chunks_pk(w, kchunks):
    return np.ascontiguousarray(w.reshape(kchunks, 128, w.shape[1]).transpose(1, 0, 2))


def prep_shared(inp, S, PAST, TT):
    NB, NT = TT // 128, S // TT
    f = lambda a: np.ascontiguousarray(np.asarray(a, dtype=np.float32))
    w_in, w_o, w_up, w_down = f(inp["w_in"][0]), f(inp["w_o"][0]), f(inp["w_up"][0]), f(inp["w_down"][0])
    ws = np.zeros((NUNITS, 128, UNIT), np.float32)
    tm = _chunks_pk(np.concatenate([w_in[:, 512:1024], w_in[:, 1408:1664], w_in[:, 1664:1728]], 1), 8)
    for u in range(4):
        ws[U_TM + u, :, :2 * 832] = tm[:, 2 * u:2 * u + 2, :].reshape(128, -1)
    fm_cols = [(i * 128, (i + 1) * 128) for i in range(4)] + [(1024 + i * 128, 1024 + (i + 1) * 128) for i in range(3)]
    for ci, (c0, c1) in enumerate(fm_cols):
        ws[U_FM + ci // 2, :, (ci % 2) * 1024:(ci % 2 + 1) * 1024] = _chunks_pk(w_in[:, c0:c1], 8).reshape(128, -1)
    wo = _chunks_pk(w_o, 8)
    for u in range(4):
        ws[U_WO + u] = wo[:, 2 * u:2 * u + 2, :].reshape(128, -1)
    for j in range(22):
        for half in range(2):
            m = j + 22 * half
            ws[U_UP + j, :, half * 1024:(half + 1) * 1024] = _chunks_pk(w_up[:, m * 128:(m + 1) * 128], 8).reshape(128, -1)
    wd = _chunks_pk(w_down, 22)
    for u in range(11):
        ws[U_DN + u] = wd[:, 2 * u:2 * u + 2, :].reshape(128, -1)
    w_uq = f(inp["w_uq"][0])
    uq = w_uq.reshape(3, 128, 4, 192).transpose(1, 0, 2, 3)
    swap = (np.arange(64) + 32) % 64
    w_uk = f(inp["w_uk"][0])
    w_uv = f(inp["w_uv"][0])
    w_s = f(inp["w_s"][0])
    bc = lambda v: np.ascontiguousarray(np.broadcast_to(f(v).reshape(1, -1), (128, f(v).size)))
    pos = np.arange(S)
    C, Sg = _rope_tables(pos)
    rtok = np.stack([C.reshape(NT, NB, 128, 64), Sg.reshape(NT, NB, 128, 64)], 3)
    rtok = np.ascontiguousarray(rtok.transpose(0, 2, 1, 3, 4)).reshape(NT, 128, NB * 2 * 64)
    rfm = np.stack([C.reshape(NT, TT, 64), Sg.reshape(NT, TT, 64)], 1)
    rfm = np.ascontiguousarray(rfm.transpose(0, 3, 1, 2)).reshape(NT, 64, 2 * TT)
    Cs, Ss = _rope_tables(PAST + np.arange(DEC))
    shared = {
        "wstream": ws,
        "wuqn": np.ascontiguousarray(uq[:, :, :, 0:128]).reshape(128, -1),
        "wuqpe": np.ascontiguousarray(uq[:, :, :, 128:192]).reshape(128, -1),
        "wuqsw": np.ascontiguousarray(uq[:, :, :, 128:192][..., swap]).reshape(128, -1),
        "wukT": np.ascontiguousarray(w_uk.transpose(2, 1, 0)).reshape(128, -1),
        "wuv": np.ascontiguousarray(w_uv.reshape(2, 128, 4, 128).transpose(1, 0, 2, 3)).reshape(128, -1),
        "wsT": np.ascontiguousarray(w_s.transpose(2, 0, 1)).reshape(128, -1),
        "ident": np.eye(128, dtype=np.float32),
        "gq": np.ascontiguousarray(f(inp["g_q"][0]).reshape(3, 128).T),
        "wconv": np.ascontiguousarray(f(inp["w_conv"][0]).reshape(3, NFC, 128).transpose(2, 1, 0)).reshape(128, -1),
        "bconv": np.ascontiguousarray(f(inp["b_conv"][0]).reshape(NFC, 128).T),
        "gkv": bc(inp["g_kv"][0]),
        "lnvg": bc(inp["ln_v_g"][0]), "lnvb": bc(inp["ln_v_b"][0]),
        "ln1g": bc(inp["ln1_g"][0]), "ln1b": bc(inp["ln1_b"][0]),
        "ln2g": bc(inp["ln2_g"][0]), "ln2b": bc(inp["ln2_b"][0]),
        "bsb": bc(inp["b_s"][0]),
        "rtok": rtok, "rfm": rfm,
        "rtok_s": np.ascontiguousarray(np.stack([Cs, Ss], 1)).reshape(DEC, 128),
        "rfm_s": np.ascontiguousarray(np.stack([Cs.T, Ss.T], 1)).reshape(64, 2 * DEC),
    }
    return shared


_NC_CACHE = {}


def run(inp, TT=256):
    f = lambda a: np.ascontiguousarray(np.asarray(a, dtype=np.float32))
    xp, xsm = f(inp["x_prompt"]), f(inp["x_sample"])
    Bn, S = xp.shape[0], xp.shape[1]
    PAST = inp["cache_ckv"].shape[2]
    key = (S, PAST, TT)
    if key not in _NC_CACHE:
        _NC_CACHE[key] = build(S, PAST, TT)
    nc = _NC_CACHE[key]
    shared = prep_shared(inp, S, PAST, TT)
    cckv, ckpe, sconv = f(inp["cache_ckv"][0]), f(inp["cache_kpe"][0]), f(inp["state_ffn_conv"][0])
    in_maps = []
    for c in range(Bn):
        m = dict(shared)
        m.update({"x": xp[c], "xs": xsm[c], "cckv": cckv[c], "ckpe": ckpe[c], "sconv": sconv[c]})
        in_maps.append(m)
    res = run_bass_kernel_spmd(nc, in_maps, core_ids=list(range(Bn)))
    r = res.results
    g = lambda name: np.stack([np.asarray(r[c][name], dtype=np.float32) for c in range(Bn)], 0)
    return (g("y"), g("ys"), g("ockv")[None], g("okpe")[None], g("oconv")[None],
            g("ockvs")[None], g("okpes")[None], g("ovs")[None], g("oconvs")[None])


def kernel(**inputs):
    return run(inputs, TT=256)
```

```python
import math
import contextlib
import numpy as np
import concourse.bass as bass
import concourse.mybir as mybir
from concourse.bass_utils import run_bass_kernel_spmd

F32 = mybir.dt.float32
BF16 = mybir.dt.bfloat16
AF = mybir.ActivationFunctionType
ALU = mybir.AluOpType
AX = mybir.AxisListType

D = 1024
GW = 512
QL = 384
KVR = 256
ROPE = 64
DFF = 2816
NFC = 44
DEC = 16
ALPHA = 2.0 ** 0.25
ATT_SCALE = 1.0 / math.sqrt(192.0)
LN_EPS = 1e-5
RMS_EPS = 1e-6
UNIT = 2048
NUNITS = 4 + 4 + 4 + 22 + 11
U_TM, U_FM, U_WO, U_UP, U_DN = 0, 4, 8, 12, 34


class Eng:
    def __init__(self, name, sem):
        self.name, self.sem, self.count, self.seen, self.prog = name, sem, 0, {}, []
        self.abs = []


class Chan:
    def __init__(self, sem):
        self.sem, self.count = sem, 0


class Dep:
    __slots__ = ("w", "r")

    def __init__(self):
        self.w, self.r = None, {}


class K:
    def __init__(self, nc, stack):
        self.nc, self.stack = nc, stack
        self.eng = {}
        for n in ("pe", "act", "dve", "pool", "sp"):
            self.eng[n] = Eng(n, stack.enter_context(nc.semaphore("e_" + n)))

    def chan(self, name):
        return Chan(self.stack.enter_context(self.nc.semaphore("c_" + name)))

    def _waits(self, e, rd, wr):
        need = {}
        for d in rd:
            if d.w is not None:
                need[d.w[0]] = max(need.get(d.w[0], 0), d.w[1])
        for d in wr:
            if d.w is not None:
                need[d.w[0]] = max(need.get(d.w[0], 0), d.w[1])
            for o, c in d.r.items():
                need[o] = max(need.get(o, 0), c)
        for o, c in need.items():
            if o is e and e.name == "pe":
                continue
            if e.seen.get(o, 0) >= c:
                continue
            e.seen[o] = c
            e.prog.append(lambda h, s=o.sem, v=c: h.wait_ge(s, v))
            e.abs.append(("wait", o, c))

    @staticmethod
    def _mark(me, rd, wr):
        for d in rd:
            d.r[me[0]] = max(d.r.get(me[0], 0), me[1])
        for d in wr:
            d.w, d.r = me, {}

    def op(self, en, fn, rd=(), wr=(), inc=True):
        e = self.eng[en]
        self._waits(e, rd, wr)
        if inc:
            e.count += 1
            e.prog.append(lambda h, f=fn, s=e.sem: f(h).then_inc(s, 1))
            e.abs.append(("inc", e, 1))
            me = (e, e.count)
        else:
            e.prog.append(lambda h, f=fn: f(h))
            me = (e, e.count + 1)
        self._mark(me, rd, wr)

    def dma(self, qn, out, in_, ch, rd=(), wr=(), slow=False):
        e = self.eng[qn]
        self._waits(e, rd, wr)
        ch.count += 16
        if slow:
            e.prog.append(lambda h, o=out, i=in_, s=ch.sem: h.dma_start(
                out=o, in_=i, allow_slow_non_contiguous=True).then_inc(s, 16))
        else:
            e.prog.append(lambda h, o=out, i=in_, s=ch.sem: h.dma_start(out=o, in_=i).then_inc(s, 16))
        e.abs.append(("inc", ch, 16))
        self._mark((ch, ch.count), rd, wr)

    def wait_all(self, qn, chans):
        e = self.eng[qn]
        for ch in chans:
            if ch.count:
                e.prog.append(lambda h, s=ch.sem, v=ch.count: h.wait_ge(s, v))

    def check_deadlock(self):
        val = {}
        pc = {n: 0 for n in self.eng}
        progress = True
        while progress:
            progress = False
            for n, e in self.eng.items():
                while pc[n] < len(e.abs):
                    kind, o, v = e.abs[pc[n]]
                    if kind == "wait":
                        if val.get(o, 0) < v:
                            break
                    else:
                        val[o] = val.get(o, 0) + v
                    pc[n] += 1
                    progress = True
        bad = False
        for n, e in self.eng.items():
            if pc[n] < len(e.abs):
                kind, o, v = e.abs[pc[n]]
                nm = o.name if isinstance(o, Eng) else "chan"
                print("DEADLOCK: engine", n, "stuck at", pc[n], "/", len(e.abs), kind, nm, v, "have", val.get(o, 0))
                bad = True
        return not bad

    def emit(self):
        with self.nc.Block() as block:
            def mk(en):
                def run(h):
                    for f in self.eng[en].prog:
                        f(h)
                return run
            block.tensor(mk("pe"))
            block.scalar(mk("act"))
            block.vector(mk("dve"))
            block.gpsimd(mk("pool"))
            block.sync(mk("sp"))


class _Stop(Exception):
    pass


STOP = None
import os
DBG_SKIP = bool(os.environ.get('DBG_SKIP'))


TILE_NO = [0]


def _chk(stage):
    if STOP == stage or STOP == "%s@%d" % (stage, TILE_NO[0]):
        raise _Stop()


class B:
    def __init__(self, t, d=None):
        self.t, self.d = t, (d if d is not None else Dep())
        self.wd = [self.d]


def build(S, PAST, TT=256, NRING=4, NBIG=4):
    NB = TT // 128
    NT = S // TT
    NKB = max(S, PAST + 128) // 128
    nc = bass.Bass("TRN2", target_bir_lowering=False)
    stack = contextlib.ExitStack()
    with stack:
        k = K(nc, stack)

        def din(name, shape, dt=F32):
            return nc.dram_tensor(name, list(shape), dt, kind="ExternalInput").ap()

        def dout(name, shape):
            return nc.dram_tensor(name, list(shape), F32, kind="ExternalOutput").ap()

        x_d = din("x", [S, D])
        xs_d = din("xs", [DEC, D])
        cckv_d = din("cckv", [PAST, KVR])
        ckpe_d = din("ckpe", [PAST, ROPE])
        sconv_d = din("sconv", [2, 2 * DFF])
        wstream_d = din("wstream", [NUNITS, 128, UNIT])
        wuqn_d = din("wuqn", [128, 3 * 4 * 128])
        wuqpe_d = din("wuqpe", [128, 3 * 4 * 64])
        wuqsw_d = din("wuqsw", [128, 3 * 4 * 64])
        wukT_d = din("wukT", [128, 4 * 256])
        wuv_d = din("wuv", [128, 2 * 4 * 128])
        wsT_d = din("wsT", [128, 4 * 128])
        ident_d = din("ident", [128, 128])
        gq_d = din("gq", [128, 3])
        wconv_d = din("wconv", [128, NFC * 3])
        bconv_d = din("bconv", [128, NFC])
        gkv_d = din("gkv", [128, KVR])
        lnvg_d = din("lnvg", [128, GW])
        lnvb_d = din("lnvb", [128, GW])
        ln1g_d = din("ln1g", [128, D])
        ln1b_d = din("ln1b", [128, D])
        ln2g_d = din("ln2g", [128, D])
        ln2b_d = din("ln2b", [128, D])
        bsb_d = din("bsb", [128, 4 * 128])
        rtok_d = din("rtok", [NT, 128, NB * 2 * 64])
        rfm_d = din("rfm", [NT, 64, 2 * TT])
        rtok_s_d = din("rtok_s", [DEC, 2 * 64])
        rfm_s_d = din("rfm_s", [64, 2 * DEC])
        wsc_d = nc.dram_tensor("wsc", [NUNITS, 128, UNIT], BF16, kind="Internal").ap()

        y_d = dout("y", [S, D])
        ys_d = dout("ys", [DEC, D])
        ockv_d = dout("ockv", [S, KVR])
        okpe_d = dout("okpe", [S, ROPE])
        oconv_d = dout("oconv", [2, 2 * DFF])
        ockvs_d = dout("ockvs", [DEC, KVR])
        okpes_d = dout("okpes", [DEC, ROPE])
        ovs_d = dout("ovs", [DEC, GW])
        oconvs_d = dout("oconvs", [2, 2 * DFF])

        def sb(name, shape, dt):
            return B(stack.enter_context(nc.sbuf_tensor("s_" + name, list(shape), dt)))

        KT = stack.enter_context(nc.sbuf_tensor("KT", [128, 3, NKB * 128], BF16))
        Vt = stack.enter_context(nc.sbuf_tensor("Vt", [128, NKB, KVR], BF16))
        kvd = [Dep() for _ in range(NKB)]
        ring = [sb("ring%d" % i, [128, UNIT], BF16) for i in range(NRING)]
        ringch = [k.chan("ring%d" % i) for i in range(NRING)]
        ringsw = [k.chan("ringsw%d" % i) for i in range(NRING)]
        big = [sb("big%d" % i, [128, D], F32) for i in range(NBIG)]
        bigch = [k.chan("big%d" % i) for i in range(NBIG)]
        bigst = [k.chan("bigst%d" % i) for i in range(NBIG)]
        wuqn = sb("wuqn", [128, 3, 4, 128], BF16)
        wuqpe = sb("wuqpe", [128, 3, 4, 64], BF16)
        wuqsw = sb("wuqsw", [128, 3, 4, 64], BF16)
        wukT = sb("wukT", [128, 4, 256], BF16)
        wuv = sb("wuv", [128, 2, 4, 128], BF16)
        wsT = sb("wsT", [128, 4, 128], BF16)
        ident = sb("ident", [128, 128], F32)
        onesb = sb("onesb", [128, 128], BF16)
        onesf = sb("onesf", [128, 128], F32)
        epsb = sb("epsb", [128, 2], F32)
        gq = sb("gq", [128, 3], F32)
        wconv = sb("wconv", [128, NFC, 3], F32)
        bconv = sb("bconv", [128, NFC], F32)
        gkv = sb("gkv", [128, KVR], F32)
        lnvg = sb("lnvg", [128, GW], F32)
        lnvb = sb("lnvb", [128, GW], F32)
        ln1g = sb("ln1g", [128, D], F32)
        ln1b = sb("ln1b", [128, D], F32)
        ln2g = sb("ln2g", [128, D], F32)
        ln2b = sb("ln2b", [128, D], F32)
        bsb = sb("bsb", [128, 4, 128], F32)
        rtok = sb("rtok", [128, NB, 2, 64], F32)
        rfm = sb("rfm", [64, 2, TT], F32)
        rtokch, rfmch = k.chan("rtok"), k.chan("rfm")
        xT = sb("xT", [128, 8, TT], BF16)
        vn = [sb("vn%d" % i, [128, GW], BF16) for i in range(NB)]
        kvf = [sb("kvf%d" % i, [128, 320], F32) for i in range(2)]
        kvfch = [k.chan("kvf%d" % i) for i in range(2)]
        kvfst = [k.chan("kvfst%d" % i) for i in range(2)]
        st = [sb("st%d" % i, [128, 8], F32) for i in range(4)]
        junk = sb("junk", [128, D], BF16)
        junkps = [sb("junkp%d" % i, [128, 64], F32) for i in range(2)]
        uT = sb("uT", [128, 4, TT], BF16)
        sq = [sb("sq%d" % i, [128, TT], F32) for i in range(3)]
        cqg = sb("cqg", [128, 3, TT], BF16)
        rb = sb("rb", [128, TT], F32)
        crb = sb("crb", [64, 2, TT], F32)
        assert TT == 256
        Rg = stack.enter_context(nc.sbuf_tensor("s_Rg", [128, 22 * TT], BF16))
        actT = B(Rg[:, :].rearrange("p (a b) -> p a b", a=22))
        qlT = B(Rg[:, 0:2048].rearrange("p (a b c) -> p a b c", a=2, b=4))
        qnT = B(Rg[:, 2048:3072].rearrange("p (a b) -> p a b", a=4))
        qpT = B(Rg[0:64, 3072:4096].rearrange("p (a b) -> p a b", a=4))
        pT = [B(Rg[:, 4096 + 512 * i:4096 + 512 * (i + 1)]) for i in range(3)]
        actc = [Dep() for _ in range(22)]
        for b_ in [qlT, qnT, qpT] + pT:
            b_.wd = [b_.d] + actc
        alias_w = [qlT.d, qnT.d, qpT.d] + [p_.d for p_ in pT]
        mixT = xT
        rinv = sb("rinv", [128, 512], F32)
        rcp = sb("rcp", [128, 512], F32)
        gtmp = B(rinv.t[:, :].rearrange("p (a b) -> p a b", a=4), rinv.d)
        onT = sb("onT", [128, 2, 512], BF16)
        hbuf = [sb("hbuf%d" % i, [128, D], F32) for i in range(NB)]
        hT = xT
        upb = [sb("upb%d" % i, [128, TT + 2], F32) for i in range(3)]
        NCV = 2
        cvas = [sb("cva%d" % i, [128, TT], F32) for i in range(NCV)]
        cvgs = [sb("cvg%d" % i, [128, TT], F32) for i in range(NCV)]
        sils = [sb("sil%d" % i, [128, TT], F32) for i in range(NCV)]
        rp1 = B(cvas[0].t[0:64, :], cvas[0].d)
        rp2 = B(cvgs[0].t[0:64, :], cvgs[0].d)
        prev2 = sb("prev2", [128, NFC, 2], F32)
        prevch = k.chan("prev2")
        prevst = k.chan("prev2st")
        ps = [B(stack.enter_context(nc.psum_tensor("ps%d" % i, [128, 512], F32))) for i in range(8)]
        setup_ch = k.chan("setup")
        setup_sw = k.chan("setupsw")
        sc_dep = [Dep() for _ in range(NUNITS)]

        state = {"rot": 0, "big": 0, "pt": 0, "upb": 0, "st": 0, "kvf": 0,
                 "unit_issued": 0, "unit_pos": 0, "qblk": 0, "srot": 0, "mrot": 0}

        def rot():
            b = ps[4 + state["rot"] % 4]
            state["rot"] += 1
            return b

        def nxt(lst, key):
            i = state[key] % len(lst)
            state[key] += 1
            return i

        setup_deps = []

        def setup_load(buf, src, cast=False):
            k.dma("pool" if cast else "sp", buf.t[:], src, setup_sw if cast else setup_ch, wr=[buf.d])
            setup_deps.append((buf.d, cast))

        setup_load(wuqn, wuqn_d.rearrange("p (a b c) -> p a b c", a=3, b=4), cast=True)
        setup_load(wuqpe, wuqpe_d.rearrange("p (a b c) -> p a b c", a=3, b=4), cast=True)
        setup_load(wuqsw, wuqsw_d.rearrange("p (a b c) -> p a b c", a=3, b=4), cast=True)
        setup_load(wukT, wukT_d.rearrange("p (a b) -> p a b", a=4), cast=True)
        setup_load(wuv, wuv_d.rearrange("p (a b c) -> p a b c", a=2, b=4), cast=True)
        setup_load(wsT, wsT_d.rearrange("p (a b) -> p a b", a=4), cast=True)
        setup_load(ident, ident_d)
        setup_load(gq, gq_d)
        setup_load(wconv, wconv_d.rearrange("p (a b) -> p a b", b=3))
        setup_load(bconv, bconv_d)
        setup_load(gkv, gkv_d)
        setup_load(lnvg, lnvg_d)
        setup_load(lnvb, lnvb_d)
        setup_load(ln1g, ln1g_d)
        setup_load(ln1b, ln1b_d)
        setup_load(ln2g, ln2g_d)
        setup_load(ln2b, ln2b_d)
        setup_load(bsb, bsb_d.rearrange("p (a b) -> p a b", a=4))
        for d, cast in setup_deps:
            d.w = (setup_sw, setup_sw.count) if cast else (setup_ch, setup_ch.count)
        k.op("pool", lambda h: h.memset(onesb.t[:], 1.0), wr=[onesb.d])
        k.op("pool", lambda h: h.memset(onesf.t[:], 1.0), wr=[onesf.d])
        k.op("pool", lambda h: h.memset(epsb.t[:, 0:1], LN_EPS), wr=[epsb.d])
        k.op("pool", lambda h: h.memset(epsb.t[:, 1:2], RMS_EPS), wr=[epsb.d])
        k.op("pool", lambda h: h.memset(wsT.t[64:128, :, 0:64], 0.0), wr=[wsT.d])

        total_units = (NT + 1) * NUNITS

        def prefetch_units(upto):
            while state["unit_issued"] < min(upto, total_units):
                pos = state["unit_issued"]
                u = pos % NUNITS
                s = pos % NRING
                if pos < NUNITS:
                    k.dma("pool", ring[s].t[:], wstream_d[u], ringsw[s], wr=[ring[s].d])
                    k.dma("sp", wsc_d[u], ring[s].t[:], ringch[s], rd=[ring[s].d], wr=[sc_dep[u]])
                else:
                    k.dma("sp", ring[s].t[:], wsc_d[u], ringch[s], rd=[sc_dep[u]], wr=[ring[s].d])
                state["unit_issued"] += 1

        def get_unit(u):
            pos = state["unit_pos"]
            assert pos % NUNITS == u, (pos, u)
            prefetch_units(pos + NRING)
            state["unit_pos"] += 1
            return ring[pos % NRING]

        def ln_multi(items, width, gam, bet):
            for _ in ln_multi_gen(items, width, gam, bet):
                pass

        def ln_multi_gen(items, width, gam, bet):
            ctx = []
            for (src, n, out_ap, out_dep) in items:
                ctx.append((src, n, out_ap, out_dep, st[nxt(st, "st")], src.t[0:n, 0:width]))
            for (src, n, out_ap, out_dep, s_, xs) in ctx:
                k.op("dve", lambda h, s_=s_, xs=xs, n=n: h.reduce_sum(s_.t[0:n, 0:1], xs, axis=AX.X),
                     rd=[src.d], wr=[s_.d])
            yield
            for (src, n, out_ap, out_dep, s_, xs) in ctx:
                k.op("dve", lambda h, s_=s_, n=n: h.tensor_scalar(s_.t[0:n, 1:2], s_.t[0:n, 0:1], -1.0 / width, None,
                                                                  op0=ALU.mult), rd=[s_.d], wr=[s_.d])
            yield
            for (src, n, out_ap, out_dep, s_, xs) in ctx:
                k.op("act", lambda h, s_=s_, xs=xs, n=n: h.activation(
                    junk.t[0:n, 0:width], xs, AF.Square, bias=s_.t[0:n, 1:2], accum_out=s_.t[0:n, 2:3]),
                    rd=[src.d, s_.d], wr=[junk.d, s_.d])
            yield
            for (src, n, out_ap, out_dep, s_, xs) in ctx:
                k.op("act", lambda h, s_=s_, n=n: h.activation(
                    s_.t[0:n, 3:4], s_.t[0:n, 2:3], AF.Sqrt, bias=epsb.t[0:n, 0:1], scale=1.0 / width),
                    rd=[s_.d, epsb.d], wr=[s_.d])
            yield
            for (src, n, out_ap, out_dep, s_, xs) in ctx:
                k.op("dve", lambda h, s_=s_, n=n: h.reciprocal(s_.t[0:n, 4:5], s_.t[0:n, 3:4]), rd=[s_.d], wr=[s_.d])
            yield
            for (src, n, out_ap, out_dep, s_, xs) in ctx:
                k.op("dve", lambda h, s_=s_, xs=xs, n=n: h.scalar_tensor_tensor(
                    xs, xs, s_.t[0:n, 1:2], gam.t[0:n, 0:width], op0=ALU.add, op1=ALU.mult),
                    rd=[src.d, s_.d, gam.d], wr=[src.d])
            yield
            for (src, n, out_ap, out_dep, s_, xs) in ctx:
                wr = [src.d] if out_dep is src.d else [out_dep]
                k.op("dve", lambda h, s_=s_, xs=xs, n=n, out_ap=out_ap: h.scalar_tensor_tensor(
                    out_ap, xs, s_.t[0:n, 4:5], bet.t[0:n, 0:width], op0=ALU.mult, op1=ALU.add),
                    rd=[src.d, s_.d, bet.d], wr=wr)
            yield

        def ln_rows(src, n, width, gam, bet, out_ap, out_dep):
            ln_multi([(src, n, out_ap, out_dep)], width, gam, bet)

        def transpose_to(dst, dst_dep, src, src_dep, n, nchunks):
            c = 0
            while c < nchunks:
                g = min(4, nchunks - c)
                pb = rot()
                for i in range(g):
                    k.op("pe", lambda h, i=i, c=c, pb=pb: h.transpose(
                        pb.t[:, i * 128:i * 128 + n], src(c + i), ident.t[0:n, 0:n]),
                        rd=[src_dep, ident.d], wr=[pb.d], inc=(i == g - 1))
                srcv = pb.t[:, 0:g * 128].rearrange("p (a b) -> p a b", a=g)[:, :, 0:n]
                k.op("act", lambda h, c=c, g=g, srcv=srcv: h.copy(dst(c, g), srcv), rd=[pb.d], wr=[dst_dep])
                c += g

        def append_kv(kb, kf, n):
            k.op("pool", lambda h: h.tensor_copy(Vt[0:n, kb, :], kf.t[0:n, 0:KVR]), rd=[kf.d], wr=[kvd[kb]])
            pb = rot()
            for i in range(3):
                w_ = 128 if i < 2 else 64
                k.op("pe", lambda h, i=i, w_=w_, pb=pb: h.transpose(
                    pb.t[0:w_, i * 128:i * 128 + n], kf.t[0:n, i * 128:i * 128 + w_], ident.t[0:n, 0:n]),
                    rd=[kf.d, ident.d], wr=[pb.d], inc=(i == 2))
            k.op("act", lambda h, pb=pb: h.copy(
                KT[:, 0:2, kb * 128:kb * 128 + n],
                pb.t[:, 0:256].rearrange("p (a b) -> p a b", a=2)[:, :, 0:n]), rd=[pb.d], wr=[kvd[kb]])
            k.op("act", lambda h, pb=pb: h.copy(KT[0:64, 2, kb * 128:kb * 128 + n], pb.t[0:64, 256:256 + n]),
                 rd=[pb.d], wr=[kvd[kb]])

        def load_x_blocks(src_d, row0, blocks):
            res = []
            for (o, n) in blocks:
                bi = nxt(big, "big")
                k.dma("sp", big[bi].t[0:n, :], src_d[row0 + o:row0 + o + n, :], bigch[bi], wr=[big[bi].d])
                res.append(bi)
            return res

        def tile(sample, t, xa_idx, next_x, b_done=False):
            n = DEC if sample else TT
            blocks = [(0, DEC)] if sample else [(b * 128, 128) for b in range(NB)]
            src_d = xs_d if sample else x_d
            row0 = 0 if sample else t * TT
            kb0 = PAST // 128 if sample else t * NB
            y_out = ys_d if sample else y_d
            ckv_out = ockvs_d if sample else ockv_d
            kpe_out = okpes_d if sample else okpe_d

            if sample:
                k.dma("sp", rtok.t[0:DEC, 0, :, :], rtok_s_d.rearrange("p (a b) -> p a b", a=2), rtokch, wr=[rtok.d])
                k.dma("sp", rfm.t[:, :, 0:DEC], rfm_s_d.rearrange("p (a b) -> p a b", a=2), rfmch, wr=[rfm.d])
            else:
                k.dma("sp", rtok.t[:], rtok_d[t].rearrange("p (a b c) -> p a b c", a=NB, b=2), rtokch, wr=[rtok.d])
                k.dma("sp", rfm.t[:], rfm_d[t].rearrange("p (a b) -> p a b", a=2), rfmch, wr=[rfm.d])

            def stage_b(idx_list, blks):
                for bi_, (o, nb_) in zip(idx_list, blks):
                    xb_ = big[bi_]
                    transpose_to(lambda c, g, o=o, nb_=nb_: xT.t[:, c:c + g, o:o + nb_], xT.d,
                                 lambda c, xb_=xb_, nb_=nb_: xb_.t[0:nb_, c * 128:(c + 1) * 128], xb_.d, nb_, 8)
            if not b_done:
                stage_b(xa_idx, blocks)

            _chk("B")
            accs = [ps[i] for i in range(4)]
            for uu in range(4):
                un = get_unit(U_TM + uu)
                w3 = un.t[:, 0:2 * 832].rearrange("p (a b) -> p a b", a=2)
                for bi, (o, nb_) in enumerate(blocks):
                    for kk in range(2):
                        kc = uu * 2 + kk
                        for g, (c0, c1) in enumerate(((0, 512), (512, 832))):
                            a_ = accs[bi * 2 + g]
                            k.op("pe", lambda h, a_=a_, kc=kc, o=o, nb_=nb_, kk=kk, c0=c0, c1=c1, w3=w3: h.matmul(
                                a_.t[0:nb_, 0:c1 - c0], xT.t[:, kc, o:o + nb_], w3[:, kk, c0:c1],
                                start=(kc == 0), stop=(kc == 7)), rd=[xT.d, un.d], wr=[a_.d],
                                inc=(kc == 7 or (bi == len(blocks) - 1 and kk == 1 and g == 1)))

            _chk("C1")
            sqs = []
            for uu in range(4):
                un = get_unit(U_FM + uu)
                w4 = un.t[:].rearrange("p (m a b) -> p m a b", m=2, a=8)
                for mm in range(2):
                    ci = uu * 2 + mm
                    if ci >= 7:
                        continue
                    pb = rot()
                    for kc in range(8):
                        k.op("pe", lambda h, pb=pb, kc=kc, mm=mm, w4=w4: h.matmul(
                            pb.t[:, 0:n], w4[:, mm, kc, :], xT.t[:, kc, 0:n], start=(kc == 0), stop=(kc == 7)),
                            rd=[xT.d, un.d], wr=[pb.d], inc=(kc == 7))
                    if ci == 0:
                        _chk("C2mm")
                    if ci < 4:
                        k.op("act", lambda h, pb=pb, ci=ci: h.activation(uT.t[:, ci, 0:n], pb.t[:, 0:n],
                                                                         AF.Gelu_apprx_tanh), rd=[pb.d], wr=[uT.d])
                        if ci == 0:
                            _chk("C2a")
                        if ci == 3:
                            _chk("C2u")
                    else:
                        cc = ci - 4
                        sqb = sq[cc]
                        if cc == 0:
                            _chk("C2q0mm")
                        k.op("act", lambda h, pb=pb, sqb=sqb: h.activation(sqb.t[:, 0:n], pb.t[:, 0:n], AF.Square),
                             rd=[pb.d], wr=[sqb.d])
                        if cc == 0:
                            _chk("C2q0sq")
                        k.op("act", lambda h, pb=pb, cc=cc: h.activation(
                            cqg.t[:, cc, 0:n], pb.t[:, 0:n], AF.Copy, scale=gq.t[:, cc:cc + 1]),
                            rd=[pb.d, gq.d], wr=[cqg.d])
                        sqs.append(sqb)
                        if cc == 0:
                            _chk("C2q0")
                        if cc == 1:
                            _chk("C2q1")

            _chk("C2")
            kfs, sts, vts = [], [], []
            for bi, (o, nb_) in enumerate(blocks):
                ki = nxt(kvf, "kvf")
                kfs.append(ki)
                sts.append(st[nxt(st, "st")])
            for bi, (o, nb_) in enumerate(blocks):
                akv, s_ = accs[bi * 2 + 1], sts[bi]
                k.op("act", lambda h, akv=akv, s_=s_, nb_=nb_: h.activation(
                    junk.t[0:nb_, 0:KVR], akv.t[0:nb_, 0:KVR], AF.Square, accum_out=s_.t[0:nb_, 0:1]),
                    rd=[akv.d], wr=[junk.d, s_.d])
            for bi, (o, nb_) in enumerate(blocks):
                s_ = sts[bi]
                k.op("act", lambda h, s_=s_, nb_=nb_: h.activation(
                    s_.t[0:nb_, 1:2], s_.t[0:nb_, 0:1], AF.Sqrt, bias=epsb.t[0:nb_, 1:2], scale=1.0 / KVR),
                    rd=[s_.d, epsb.d], wr=[s_.d])
            for bi, (o, nb_) in enumerate(blocks):
                akv, kf, jp = accs[bi * 2 + 1], kvf[kfs[bi]], junkps[bi % 2]
                k.op("dve", lambda h, kf=kf, akv=akv, bi=bi, nb_=nb_: h.tensor_tensor(
                    kf.t[0:nb_, 256:320], akv.t[0:nb_, 256:320], rtok.t[0:nb_, bi, 0, :], op=ALU.mult),
                    rd=[akv.d, rtok.d], wr=[kf.d])
                k.op("dve", lambda h, akv=akv, bi=bi, nb_=nb_, jp=jp: h.tensor_tensor(
                    jp.t[0:nb_, 0:32], akv.t[0:nb_, 288:320], rtok.t[0:nb_, bi, 1, 0:32], op=ALU.mult),
                    rd=[akv.d, rtok.d], wr=[jp.d])
                k.op("dve", lambda h, akv=akv, bi=bi, nb_=nb_, jp=jp: h.tensor_tensor(
                    jp.t[0:nb_, 32:64], akv.t[0:nb_, 256:288], rtok.t[0:nb_, bi, 1, 32:64], op=ALU.mult),
                    rd=[akv.d, rtok.d], wr=[jp.d])
                k.op("dve", lambda h, kf=kf, nb_=nb_, jp=jp: h.tensor_tensor(
                    kf.t[0:nb_, 256:320], kf.t[0:nb_, 256:320], jp.t[0:nb_, :], op=ALU.add),
                    rd=[kf.d, jp.d], wr=[kf.d])
            for bi, (o, nb_) in enumerate(blocks):
                s_ = sts[bi]
                k.op("dve", lambda h, s_=s_, nb_=nb_: h.reciprocal(s_.t[0:nb_, 2:3], s_.t[0:nb_, 1:2]),
                     rd=[s_.d], wr=[s_.d])
            for bi, (o, nb_) in enumerate(blocks):
                akv, kf, s_ = accs[bi * 2 + 1], kvf[kfs[bi]], sts[bi]
                k.op("dve", lambda h, kf=kf, akv=akv, s_=s_, nb_=nb_: h.scalar_tensor_tensor(
                    kf.t[0:nb_, 0:KVR], akv.t[0:nb_, 0:KVR], s_.t[0:nb_, 2:3], gkv.t[0:nb_, :],
                    op0=ALU.mult, op1=ALU.mult), rd=[akv.d, s_.d, gkv.d], wr=[kf.d])
            for bi, (o, nb_) in enumerate(blocks):
                av = accs[bi * 2]
                vi = nxt(big, "big")
                vts.append(vi)
                vt = big[vi]
                k.op("act", lambda h, vt=vt, av=av, nb_=nb_: h.activation(vt.t[0:nb_, 0:GW], av.t[0:nb_, :],
                                                                          AF.Gelu_apprx_tanh), rd=[av.d], wr=[vt.d])
            for bi, (o, nb_) in enumerate(blocks):
                kf, ki = kvf[kfs[bi]], kfs[bi]
                r0 = row0 + o
                k.dma("pool", ckv_out[r0:r0 + nb_, :], kf.t[0:nb_, 0:KVR], kvfst[ki], rd=[kf.d])
                k.dma("pool", kpe_out[r0:r0 + nb_, :], kf.t[0:nb_, 256:320], kvfst[ki], rd=[kf.d])
                append_kv(kb0 + bi, kf, nb_)
            def c3b_gen():
                yield from ln_multi_gen([(big[vts[bi]], nb_, big[vts[bi]].t[0:nb_, 0:GW], big[vts[bi]].d)
                                         for bi, (o, nb_) in enumerate(blocks)], GW, lnvg, lnvb)
                for bi, (o, nb_) in enumerate(blocks):
                    vt = big[vts[bi]]
                    k.op("pool", lambda h, vt=vt, bi=bi, nb_=nb_: h.tensor_copy(vn[bi].t[0:nb_, :], vt.t[0:nb_, 0:GW]),
                         rd=[vt.d], wr=[vn[bi].d])
                    if sample:
                        k.dma("pool", ovs_d[0:nb_, :], vt.t[0:nb_, 0:GW], bigst[vts[bi]], rd=[vt.d])

            _chk("C3")
            def stage_d():
                for bi, (o, nb_) in enumerate(blocks):
                    pb = rot()
                    for hh in range(4):
                        k.op("pe", lambda h, pb=pb, hh=hh, bi=bi, nb_=nb_: h.matmul(
                            pb.t[:, hh * 128:hh * 128 + nb_], vn[bi].t[0:nb_, hh * 128:(hh + 1) * 128],
                            wsT.t[0:nb_, hh, 0:nb_], start=True, stop=True),
                            rd=[vn[bi].d, wsT.d], wr=[pb.d], inc=(hh == 3))
                    k.op("dve", lambda h, pb=pb, nb_=nb_: h.tensor_tensor(
                        gtmp.t[:, :, 0:nb_], pb.t[:, 0:512].rearrange("p (a b) -> p a b", a=4)[:, :, 0:nb_],
                        bsb.t[:, :, 0:nb_], op=ALU.add), rd=[pb.d, bsb.d], wr=[gtmp.d])
                    k.op("dve", lambda h, o=o, nb_=nb_: h.tensor_tensor(
                        mixT.t[:, 0:4, o:o + nb_], gtmp.t[:, :, 0:nb_], uT.t[:, :, o:o + nb_], op=ALU.mult),
                        rd=[gtmp.d, uT.d], wr=[mixT.d])

            _chk("D")
            rs = rot()
            for cc in range(3):
                k.op("pe", lambda h, cc=cc, rs=rs: h.matmul(rs.t[:, 0:n], onesf.t[:], sq[cc].t[:, 0:n],
                                                            start=(cc == 0), stop=(cc == 2)),
                     rd=[sq[cc].d, onesf.d], wr=[rs.d], inc=(cc == 2))
            k.op("act", lambda h, rs=rs: h.activation(rb.t[:, 0:n], rs.t[:, 0:n], AF.Sqrt, bias=epsb.t[:, 1:2],
                                                      scale=1.0 / QL), rd=[rs.d, epsb.d], wr=[rb.d])
            k.op("dve", lambda h: h.reciprocal(rb.t[:, 0:n], rb.t[:, 0:n]), rd=[rb.d], wr=[rb.d])
            for tb in range(2):
                k.op("dve", lambda h, tb=tb: h.tensor_tensor(
                    crb.t[:, tb, 0:n], rfm.t[:, tb, 0:n], rb.t[0:64, 0:n], op=ALU.mult),
                    rd=[rfm.d, rb.d], wr=[crb.d])
            for hh in range(4):
                pb = rot()
                for rk in range(3):
                    k.op("pe", lambda h, pb=pb, rk=rk, hh=hh: h.matmul(
                        pb.t[:, 0:n], wuqn.t[:, rk, hh, :], cqg.t[:, rk, 0:n], start=(rk == 0), stop=(rk == 2)),
                        rd=[wuqn.d, cqg.d], wr=[pb.d], inc=(rk == 2))
                k.op("dve", lambda h, pb=pb, hh=hh: h.tensor_tensor(qnT.t[:, hh, 0:n], pb.t[:, 0:n], rb.t[:, 0:n],
                                                                    op=ALU.mult), rd=[pb.d, rb.d], wr=qnT.wd)
            for hh in range(4):
                pbx, pbs = rot(), rot()
                for (pb_, w_) in ((pbx, wuqpe), (pbs, wuqsw)):
                    for rk in range(3):
                        k.op("pe", lambda h, pb_=pb_, w_=w_, rk=rk, hh=hh: h.matmul(
                            pb_.t[0:64, 0:n], w_.t[:, rk, hh, :], cqg.t[:, rk, 0:n], start=(rk == 0), stop=(rk == 2)),
                            rd=[w_.d, cqg.d], wr=[pb_.d], inc=(rk == 2))
                k.op("dve", lambda h, pbx=pbx: h.tensor_tensor(rp1.t[:, 0:n], pbx.t[0:64, 0:n], crb.t[:, 0, 0:n],
                                                               op=ALU.mult), rd=[pbx.d, crb.d], wr=[rp1.d])
                k.op("dve", lambda h, pbs=pbs: h.tensor_tensor(rp2.t[:, 0:n], pbs.t[0:64, 0:n], crb.t[:, 1, 0:n],
                                                               op=ALU.mult), rd=[pbs.d, crb.d], wr=[rp2.d])
                k.op("pool", lambda h, hh=hh: h.tensor_tensor(qpT.t[:, hh, 0:n], rp1.t[:, 0:n], rp2.t[:, 0:n],
                                                              op=ALU.add), rd=[rp1.d, rp2.d], wr=qpT.wd)
            for hh in range(4):
                for cc in range(2):
                    pb = rot()
                    k.op("pe", lambda h, pb=pb, hh=hh, cc=cc: h.matmul(
                        pb.t[:, 0:n], wukT.t[:, hh, cc * 128:(cc + 1) * 128], qnT.t[:, hh, 0:n],
                        start=True, stop=True), rd=[wukT.d, qnT.d], wr=[pb.d])
                    k.op("act", lambda h, pb=pb, hh=hh, cc=cc: h.copy(qlT.t[:, cc, hh, 0:n], pb.t[:, 0:n]),
                         rd=[pb.d], wr=qlT.wd)

            _chk("F")
            xb_idx = load_x_blocks(src_d, row0, blocks)

            steps = []
            for bi, (o, nq) in enumerate(blocks):
                own = kb0 + bi
                nkeys = [(kb, 128) for kb in range(own)] + [(own, nq)]
                qg = state["qblk"]
                state["qblk"] += 1
                for j_, (kb, kn) in enumerate(nkeys):
                    steps.append(dict(bi=bi, o=o, nq=nq, kb=kb, kn=kn, first=(j_ == 0), last=(j_ == len(nkeys) - 1),
                                      own=own, acc=[ps[c_] for c_ in range(3)]))

            def emit_scores(stp):
                o, nq, kb, kn = stp["o"], stp["nq"], stp["kb"], stp["kn"]
                W = 4 * nq
                sp_ = ps[3 + state["srot"] % 3]
                state["srot"] += 1
                spv = sp_.t[0:kn, 0:W].rearrange("p (a b) -> p a b", a=4)
                for cc in range(3):
                    if cc < 2:
                        lhs = KT[:, cc, kb * 128:kb * 128 + kn]
                        rhs = qlT.t[:, cc, :, o:o + nq]
                    else:
                        lhs = KT[0:64, 2, kb * 128:kb * 128 + kn]
                        rhs = qpT.t[:, :, o:o + nq]
                    k.op("pe", lambda h, spv=spv, lhs=lhs, rhs=rhs, cc=cc: h.matmul(
                        spv, lhs, rhs, start=(cc == 0), stop=(cc == 2)),
                        rd=[kvd[kb], qlT.d, qpT.d], wr=[sp_.d], inc=(cc == 2))
                pt = pT[nxt(pT, "pt")]
                if (not sample) and kb == stp["own"]:
                    k.op("pool", lambda h, pt=pt: h.memset(
                        pt.t[64:128, :].rearrange("p (a b) -> p a b", a=4)[:, :, 0:64], 0.0), wr=pt.wd)
                    k.op("act", lambda h, pt=pt, sp_=sp_, W=W: h.activation(
                        pt.t[0:64, 0:W], sp_.t[0:64, 0:W], AF.Exp, scale=ATT_SCALE), rd=[sp_.d], wr=pt.wd)
                    k.op("act", lambda h, pt=pt, sp_=sp_: h.activation(
                        pt.t[64:128, :].rearrange("p (a b) -> p a b", a=4)[:, :, 64:128],
                        sp_.t[64:128, :].rearrange("p (a b) -> p a b", a=4)[:, :, 64:128],
                        AF.Exp, scale=ATT_SCALE), rd=[sp_.d], wr=pt.wd)
                else:
                    k.op("act", lambda h, pt=pt, sp_=sp_, kn=kn, W=W: h.activation(
                        pt.t[0:kn, 0:W], sp_.t[0:kn, 0:W], AF.Exp, scale=ATT_SCALE), rd=[sp_.d], wr=pt.wd)
                stp["pt"] = pt

            def emit_pv(stp):
                nq, kb, kn, first, last, acc = stp["nq"], stp["kb"], stp["kn"], stp["first"], stp["last"], stp["acc"]
                W = 4 * nq
                pt = stp["pt"]
                for cc in range(2):
                    k.op("pe", lambda h, pt=pt, kb=kb, kn=kn, cc=cc, first=first, last=last, acc=acc, W=W: h.matmul(
                        acc[cc].t[:, 0:W], Vt[0:kn, kb, cc * 128:(cc + 1) * 128], pt.t[0:kn, 0:W],
                        start=first, stop=last), rd=[kvd[kb], pt.d], wr=[acc[cc].d], inc=(cc == 1))
                if first:
                    k.op("dve", lambda h, pt=pt, W=W: h.tensor_copy(rinv.t[:, 0:W], pt.t[:, 0:W]),
                         rd=[pt.d], wr=[rinv.d])
                else:
                    k.op("dve", lambda h, pt=pt, kn=kn, W=W: h.tensor_tensor(
                        rinv.t[0:kn, 0:W], rinv.t[0:kn, 0:W], pt.t[0:kn, 0:W], op=ALU.add),
                        rd=[pt.d, rinv.d], wr=[rinv.d])
                if last:
                    k.op("pe", lambda h, acc=acc, W=W: h.matmul(
                        acc[2].t[:, 0:W], onesf.t[:], rinv.t[:, 0:W], start=True, stop=True),
                        rd=[onesf.d, rinv.d], wr=[acc[2].d], inc=True)

            def emit_norm(stp):
                acc, W = stp["acc"], 4 * stp["nq"]
                for cc in range(2):
                    k.op("act", lambda h, cc=cc, acc=acc, W=W: h.copy(onT.t[:, cc, 0:W], acc[cc].t[:, 0:W]),
                         rd=[acc[cc].d], wr=[onT.d])
                k.op("dve", lambda h, acc=acc, W=W: h.reciprocal(rcp.t[:, 0:W], acc[2].t[:, 0:W]),
                     rd=[acc[2].d], wr=[rcp.d])

            def emit_mla(stp):
                o, nq = stp["o"], stp["nq"]
                W = 4 * nq
                pb = ps[6 + state["mrot"] % 2]
                state["mrot"] += 1
                for hh in range(4):
                    for cc in range(2):
                        k.op("pe", lambda h, pb=pb, hh=hh, cc=cc, nq=nq: h.matmul(
                            pb.t[:, hh * nq:(hh + 1) * nq], wuv.t[:, cc, hh, :], onT.t[:, cc, hh * nq:(hh + 1) * nq],
                            start=(cc == 0), stop=(cc == 1)), rd=[wuv.d, onT.d], wr=[pb.d],
                            inc=(hh == 3 and cc == 1))
                k.op("dve", lambda h, pb=pb, o=o, nq=nq, W=W: h.tensor_tensor(
                    mixT.t[:, 4:8, o:o + nq], pb.t[:, 0:W].rearrange("p (a b) -> p a b", a=4),
                    rcp.t[:, 0:W].rearrange("p (a b) -> p a b", a=4), op=ALU.mult),
                    rd=[pb.d, rcp.d], wr=[mixT.d])

            c3b_it = [None]
            pending = None
            SKEW = 2
            for si in range(len(steps) + SKEW):
                if si < len(steps):
                    emit_scores(steps[si])
                if si >= SKEW:
                    prev = steps[si - SKEW]
                    emit_pv(prev)
                    if pending is not None:
                        pending[1] -= 1
                        if pending[1] <= 0:
                            emit_mla(pending[0])
                            pending = None
                    if prev["last"]:
                        if pending is not None:
                            emit_mla(pending[0])
                        emit_norm(prev)
                        pending = [prev, 8]
                        if c3b_it[0] is None:
                            c3b_it[0] = c3b_gen()
                    elif c3b_it[0] is not None:
                        next(c3b_it[0], None)
            if pending is not None:
                emit_mla(pending[0])
            if c3b_it[0] is None:
                c3b_it[0] = c3b_gen()
            for _ in c3b_it[0]:
                pass
            stage_d()

            _chk("G")
            for uu in range(4):
                un = get_unit(U_WO + uu)
                w3 = un.t[:].rearrange("p (a b) -> p a b", a=2)
                for bi, (o, nb_) in enumerate(blocks):
                    for kk in range(2):
                        kc = uu * 2 + kk
                        for g in range(2):
                            a_ = accs[bi * 2 + g]
                            k.op("pe", lambda h, a_=a_, kc=kc, o=o, nb_=nb_, kk=kk, g=g, w3=w3: h.matmul(
                                a_.t[0:nb_, :], mixT.t[:, kc, o:o + nb_], w3[:, kk, g * 512:(g + 1) * 512],
                                start=(kc == 0), stop=(kc == 7)), rd=[mixT.d, un.d], wr=[a_.d],
                                inc=(kc == 7 or (bi == len(blocks) - 1 and kk == 1 and g == 1)))
            for bi, (o, nb_) in enumerate(blocks):
                xb_ = big[xb_idx[bi]]
                for g in range(2):
                    a_ = accs[bi * 2 + g]
                    k.op("dve", lambda h, xb_=xb_, a_=a_, g=g, nb_=nb_: h.scalar_tensor_tensor(
                        xb_.t[0:nb_, g * 512:(g + 1) * 512], xb_.t[0:nb_, g * 512:(g + 1) * 512], ALPHA,
                        a_.t[0:nb_, :], op0=ALU.mult, op1=ALU.add), rd=[xb_.d, a_.d], wr=[xb_.d])
            ln_multi([(big[xb_idx[bi]], nb_, hbuf[bi].t[0:nb_, :], hbuf[bi].d) for bi, (o, nb_) in enumerate(blocks)],
                     D, ln1g, ln1b)
            for bi, (o, nb_) in enumerate(blocks):
                hb = hbuf[bi]
                transpose_to(lambda c, g, o=o, nb_=nb_: hT.t[:, c:c + g, o:o + nb_], hT.d,
                             lambda c, hb=hb, nb_=nb_: hb.t[0:nb_, c * 128:(c + 1) * 128], hb.d, nb_, 8)

            _chk("H")
            nxt_idx = next_x() if next_x is not None else None

            gate_pending = None
            for j in range(22):
                cva, cvg, sil = cvas[j % NCV], cvgs[j % NCV], sils[j % NCV]
                un = get_unit(U_UP + j)
                w4 = un.t[:].rearrange("p (m a b) -> p m a b", m=2, a=8)
                for half in range(2):
                    m = j + 22 * half
                    pb = rot()
                    for kc in range(8):
                        k.op("pe", lambda h, pb=pb, kc=kc, half=half, w4=w4: h.matmul(
                            pb.t[:, 0:n], w4[:, half, kc, :], hT.t[:, kc, 0:n], start=(kc == 0), stop=(kc == 7)),
                            rd=[hT.d, un.d], wr=[pb.d], inc=(kc == 7))
                    ub = upb[nxt(upb, "upb")]
                    k.op("pool", lambda h, ub=ub, m=m: h.tensor_copy(ub.t[:, 0:2], prev2.t[:, m, :]),
                         rd=[prev2.d], wr=[ub.d])
                    k.op("act", lambda h, ub=ub, pb=pb: h.copy(ub.t[:, 2:2 + n], pb.t[:, 0:n]), rd=[pb.d], wr=[ub.d])
                    k.op("pool", lambda h, ub=ub, m=m: h.tensor_copy(prev2.t[:, m, :], ub.t[:, n:n + 2]),
                         rd=[ub.d], wr=[prev2.d])
                    cv = cva if half == 0 else cvg
                    k.op("act", lambda h, pb=pb, cv=cv, m=m: h.activation(
                        cv.t[:, 0:n], pb.t[:, 0:n], AF.Identity, bias=bconv.t[:, m:m + 1], scale=wconv.t[:, m, 2:3]),
                        rd=[pb.d, wconv.d, bconv.d], wr=[cv.d])
                    for tap in (0, 1):
                        k.op("dve", lambda h, ub=ub, cv=cv, m=m, tap=tap: h.scalar_tensor_tensor(
                            cv.t[:, 0:n], ub.t[:, tap:tap + n], wconv.t[:, m, tap:tap + 1], cv.t[:, 0:n],
                            op0=ALU.mult, op1=ALU.add), rd=[ub.d, wconv.d, cv.d], wr=[cv.d])
                if gate_pending is not None:
                    gate_pending()

                def gate(j=j, sil=sil, cva=cva, cvg=cvg):
                    k.op("act", lambda h: h.activation(sil.t[:, 0:n], cva.t[:, 0:n], AF.Silu),
                         rd=[cva.d], wr=[sil.d])
                    k.op("dve", lambda h: h.tensor_tensor(
                        actT.t[:, j, 0:n], sil.t[:, 0:n], cvg.t[:, 0:n], op=ALU.mult),
                        rd=[sil.d, cvg.d], wr=[actc[j]] + alias_w)
                gate_pending = gate
            gate_pending()

            _chk("I")
            for uu in range(11):
                un = get_unit(U_DN + uu)
                w3 = un.t[:].rearrange("p (a b) -> p a b", a=2)
                for bi, (o, nb_) in enumerate(blocks):
                    for kk in range(2):
                        kc = uu * 2 + kk
                        for g in range(2):
                            a_ = accs[bi * 2 + g]
                            k.op("pe", lambda h, a_=a_, kc=kc, o=o, nb_=nb_, kk=kk, g=g, w3=w3: h.matmul(
                                a_.t[0:nb_, :], actT.t[:, kc, o:o + nb_], w3[:, kk, g * 512:(g + 1) * 512],
                                start=(kc == 0), stop=(kc == 21)), rd=[actc[kc], un.d], wr=[a_.d],
                                inc=(kc == 21 or (bi == len(blocks) - 1 and kk == 1 and g == 1)))
            if nxt_idx is not None:
                stage_b(nxt_idx, [(b * 128, 128) for b in range(NB)])
            yis = []
            for bi, (o, nb_) in enumerate(blocks):
                yi = nxt(big, "big")
                yis.append(yi)
                yb = big[yi]
                hb = hbuf[bi]
                for g in range(2):
                    a_ = accs[bi * 2 + g]
                    k.op("dve", lambda h, yb=yb, hb=hb, a_=a_, g=g, nb_=nb_: h.scalar_tensor_tensor(
                        yb.t[0:nb_, g * 512:(g + 1) * 512], hb.t[0:nb_, g * 512:(g + 1) * 512], ALPHA,
                        a_.t[0:nb_, :], op0=ALU.mult, op1=ALU.add), rd=[hb.d, a_.d], wr=[yb.d])
            ln_multi([(big[yis[bi]], nb_, big[yis[bi]].t[0:nb_, :], big[yis[bi]].d)
                      for bi, (o, nb_) in enumerate(blocks)], D, ln2g, ln2b)
            for bi, (o, nb_) in enumerate(blocks):
                r0 = row0 + o
                k.dma("pool", y_out[r0:r0 + nb_, :], big[yis[bi]].t[0:nb_, :], bigst[yis[bi]], rd=[big[yis[bi]].d])
            return nxt_idx

        def main_program():
            for part in range(4):
                for tt_ in range(2):
                    k.dma("sp", prev2.t[:, part * 11:(part + 1) * 11, tt_],
                          sconv_d[tt_, part * 1408:(part + 1) * 1408].rearrange("(m p) -> p m", p=128),
                          prevch, wr=[prev2.d], slow=True)
            for kb in range(PAST // 128):
                ki = nxt(kvf, "kvf")
                kf = kvf[ki]
                k.dma("sp", kf.t[:, 0:KVR], cckv_d[kb * 128:(kb + 1) * 128, :], kvfch[ki], wr=[kf.d])
                k.dma("sp", kf.t[:, 256:320], ckpe_d[kb * 128:(kb + 1) * 128, :], kvfch[ki], wr=[kf.d])
                append_kv(kb, kf, 128)
            _chk("cache")
            xs_idx = load_x_blocks(xs_d, 0, [(0, DEC)])
            TILE_NO[0] = 0
            x0_idx = tile(True, 0, xs_idx, lambda: load_x_blocks(x_d, 0, [(b * 128, 128) for b in range(NB)]))
            _chk("sample_done")
            for part in range(0 if DBG_SKIP else 4):
                for tt_ in range(2):
                    k.dma("pool", oconvs_d[tt_, part * 1408:(part + 1) * 1408].rearrange("(m p) -> p m", p=128),
                          prev2.t[:, part * 11:(part + 1) * 11, tt_], prevst, rd=[prev2.d], slow=True)
            k.op("pool", lambda h: h.memset(prev2.t[:], 0.0), wr=[prev2.d])
            cur = x0_idx
            for t in range(NT):
                if t + 1 < NT:
                    nx = (lambda t=t: load_x_blocks(x_d, (t + 1) * TT, [(b * 128, 128) for b in range(NB)]))
                else:
                    nx = None
                TILE_NO[0] = t + 1
                cur = tile(False, t, cur, nx, b_done=True)
                _chk("tile_done")
            for part in range(4):
                for tt_ in range(2):
                    k.dma("pool", oconv_d[tt_, part * 1408:(part + 1) * 1408].rearrange("(m p) -> p m", p=128),
                          prev2.t[:, part * 11:(part + 1) * 11, tt_], prevst, rd=[prev2.d], slow=True)
        try:
            _chk("setup")
            main_program()
        except _Stop:
            pass
        k.wait_all("pool", bigch + bigst + kvfch + kvfst + [prevch, prevst] + ringch + ringsw
                   + [setup_ch, setup_sw, rtokch, rfmch])
        assert k.check_deadlock(), "build-time deadlock check failed"
        k.emit()
    return nc


def _rope_tables(pos):
    inv = np.power(np.float32(10000.0), -np.arange(0, ROPE, 2, dtype=np.float32) / np.float32(ROPE)).astype(np.float32)
    ang = pos.astype(np.float32)[:, None] * inv[None, :]
    c, s = np.cos(ang).astype(np.float32), np.sin(ang).astype(np.float32)
    return np.concatenate([c, c], 1), np.concatenate([-s, s], 1)


def _chunks_pk(w, kchunks):
    return np.ascontiguousarray(w.reshape(kchunks, 128, w.shape[1]).transpose(1, 0, 2))


def prep_shared(inp, S, PAST, TT):
    NB, NT = TT // 128, S // TT
    f = lambda a: np.ascontiguousarray(np.asarray(a, dtype=np.float32))
    w_in, w_o, w_up, w_down = f(inp["w_in"][0]), f(inp["w_o"][0]), f(inp["w_up"][0]), f(inp["w_down"][0])
    ws = np.zeros((NUNITS, 128, UNIT), np.float32)
    tm = _chunks_pk(np.concatenate([w_in[:, 512:1024], w_in[:, 1408:1664], w_in[:, 1664:1728]], 1), 8)
    for u in range(4):
        ws[U_TM + u, :, :2 * 832] = tm[:, 2 * u:2 * u + 2, :].reshape(128, -1)
    fm_cols = [(i * 128, (i + 1) * 128) for i in range(4)] + [(1024 + i * 128, 1024 + (i + 1) * 128) for i in range(3)]
    for ci, (c0, c1) in enumerate(fm_cols):
        ws[U_FM + ci // 2, :, (ci % 2) * 1024:(ci % 2 + 1) * 1024] = _chunks_pk(w_in[:, c0:c1], 8).reshape(128, -1)
    wo = _chunks_pk(w_o, 8)
    for u in range(4):
        ws[U_WO + u] = wo[:, 2 * u:2 * u + 2, :].reshape(128, -1)
    for j in range(22):
        for half in range(2):
            m = j + 22 * half
            ws[U_UP + j, :, half * 1024:(half + 1) * 1024] = _chunks_pk(w_up[:, m * 128:(m + 1) * 128], 8).reshape(128, -1)
    wd = _chunks_pk(w_down, 22)
    for u in range(11):
        ws[U_DN + u] = wd[:, 2 * u:2 * u + 2, :].reshape(128, -1)
    w_uq = f(inp["w_uq"][0])
    uq = w_uq.reshape(3, 128, 4, 192).transpose(1, 0, 2, 3)
    swap = (np.arange(64) + 32) % 64
    w_uk = f(inp["w_uk"][0])
    w_uv = f(inp["w_uv"][0])
    w_s = f(inp["w_s"][0])
    bc = lambda v: np.ascontiguousarray(np.broadcast_to(f(v).reshape(1, -1), (128, f(v).size)))
    pos = np.arange(S)
    C, Sg = _rope_tables(pos)
    rtok = np.stack([C.reshape(NT, NB, 128, 64), Sg.reshape(NT, NB, 128, 64)], 3)
    rtok = np.ascontiguousarray(rtok.transpose(0, 2, 1, 3, 4)).reshape(NT, 128, NB * 2 * 64)
    rfm = np.stack([C.reshape(NT, TT, 64), Sg.reshape(NT, TT, 64)], 1)
    rfm = np.ascontiguousarray(rfm.transpose(0, 3, 1, 2)).reshape(NT, 64, 2 * TT)
    Cs, Ss = _rope_tables(PAST + np.arange(DEC))
    shared = {
        "wstream": ws,
        "wuqn": np.ascontiguousarray(uq[:, :, :, 0:128]).reshape(128, -1),
        "wuqpe": np.ascontiguousarray(uq[:, :, :, 128:192]).reshape(128, -1),
        "wuqsw": np.ascontiguousarray(uq[:, :, :, 128:192][..., swap]).reshape(128, -1),
        "wukT": np.ascontiguousarray(w_uk.transpose(2, 1, 0)).reshape(128, -1),
        "wuv": np.ascontiguousarray(w_uv.reshape(2, 128, 4, 128).transpose(1, 0, 2, 3)).reshape(128, -1),
        "wsT": np.ascontiguousarray(w_s.transpose(2, 0, 1)).reshape(128, -1),
        "ident": np.eye(128, dtype=np.float32),
        "gq": np.ascontiguousarray(f(inp["g_q"][0]).reshape(3, 128).T),
        "wconv": np.ascontiguousarray(f(inp["w_conv"][0]).reshape(3, NFC, 128).transpose(2, 1, 0)).reshape(128, -1),
        "bconv": np.ascontiguousarray(f(inp["b_conv"][0]).reshape(NFC, 128).T),
        "gkv": bc(inp["g_kv"][0]),
        "lnvg": bc(inp["ln_v_g"][0]), "lnvb": bc(inp["ln_v_b"][0]),
        "ln1g": bc(inp["ln1_g"][0]), "ln1b": bc(inp["ln1_b"][0]),
        "ln2g": bc(inp["ln2_g"][0]), "ln2b": bc(inp["ln2_b"][0]),
        "bsb": bc(inp["b_s"][0]),
        "rtok": rtok, "rfm": rfm,
        "rtok_s": np.ascontiguousarray(np.stack([Cs, Ss], 1)).reshape(DEC, 128),
        "rfm_s": np.ascontiguousarray(np.stack([Cs.T, Ss.T], 1)).reshape(64, 2 * DEC),
    }
    return shared


_NC_CACHE = {}


def run(inp, TT=256):
    f = lambda a: np.ascontiguousarray(np.asarray(a, dtype=np.float32))
    xp, xsm = f(inp["x_prompt"]), f(inp["x_sample"])
    Bn, S = xp.shape[0], xp.shape[1]
    PAST = inp["cache_ckv"].shape[2]
    key = (S, PAST, TT)
    if key not in _NC_CACHE:
        _NC_CACHE[key] = build(S, PAST, TT)
    nc = _NC_CACHE[key]
    shared = prep_shared(inp, S, PAST, TT)
    cckv, ckpe, sconv = f(inp["cache_ckv"][0]), f(inp["cache_kpe"][0]), f(inp["state_ffn_conv"][0])
    in_maps = []
    for c in range(Bn):
        m = dict(shared)
        m.update({"x": xp[c], "xs": xsm[c], "cckv": cckv[c], "ckpe": ckpe[c], "sconv": sconv[c]})
        in_maps.append(m)
    res = run_bass_kernel_spmd(nc, in_maps, core_ids=list(range(Bn)))
    r = res.results
    g = lambda name: np.stack([np.asarray(r[c][name], dtype=np.float32) for c in range(Bn)], 0)
    return (g("y"), g("ys"), g("ockv")[None], g("okpe")[None], g("oconv")[None],
            g("ockvs")[None], g("okpes")[None], g("ovs")[None], g("oconvs")[None])


def kernel(**inputs):
    return run(inputs, TT=256)
```
